# Optimizing a Trainium2 kernel written in Bass

```python
import jax, jax.numpy as jnp
from jax import lax
import numpy as np

D_MODEL = 1024
BATCH = 2
SEQ = 16384
DEPTH = 4

CHUNK = 64
N_MIXERS = 2
N_A = (DEPTH + 1) // 2
N_B = DEPTH // 2
CONV_WIDTH = 31
CONV_INNER = D_MODEL
POOL_INNER = D_MODEL
POOL_WINDOWS = (2, 4, 8, 16)
POOL_GROUPS = len(POOL_WINDOWS)
POOL_GC = POOL_INNER // POOL_GROUPS
RMS_EPS = 1e-6
LN_EPS = 1e-5

kernel_name = "hybrid_conformer_conv_multiscale_pool_trunk"


def rmsnorm(x, g):
    xf = x.astype(jnp.float32)
    y = xf * lax.rsqrt(jnp.mean(xf * xf, axis=-1, keepdims=True) + RMS_EPS)
    return (y * g.astype(jnp.float32)).astype(x.dtype)


def layernorm(x, g, b):
    xf = x.astype(jnp.float32)
    mu = jnp.mean(xf, axis=-1, keepdims=True)
    var = jnp.mean(jnp.square(xf - mu), axis=-1, keepdims=True)
    y = (xf - mu) * lax.rsqrt(var + LN_EPS)
    return (y * g.astype(jnp.float32) + b.astype(jnp.float32)).astype(x.dtype)


def conformer_conv_branch(h, w_in, dw, dw_b, ln_g, ln_b, w_out):
    p = jnp.einsum('bsd,de->bse', h, w_in)
    a, b, z = jnp.split(p, 3, axis=-1)
    u = a * jax.nn.sigmoid(b)
    u = lax.conv_general_dilated(
        u, dw[:, None, :].astype(u.dtype), window_strides=(1,),
        padding=[(CONV_WIDTH - 1, 0)],
        dimension_numbers=('NWC', 'WIO', 'NWC'),
        feature_group_count=CONV_INNER) + dw_b
    u = layernorm(u, ln_g, ln_b)
    u = jax.nn.silu(u) * jax.nn.silu(z)
    return jnp.einsum('bse,ed->bsd', u, w_out)


def multiscale_pool_branch(h, w_in, w_grp, b_grp, scale, w_out):
    p = jnp.einsum('bsd,de->bse', h, w_in)
    u, z = jnp.split(p, 2, axis=-1)
    S = u.shape[1]
    uf = u.astype(jnp.float32)
    cs = jnp.pad(jnp.cumsum(uf, axis=1), ((0, 0), (1, 0), (0, 0)))
    pos = jnp.arange(S, dtype=jnp.int32) + 1
    outs = []
    for g, w in enumerate(POOL_WINDOWS):
        sl = slice(g * POOL_GC, (g + 1) * POOL_GC)
        csg = cs[:, :, sl]
        upper = csg[:, 1:]
        lower = jnp.pad(csg[:, :S + 1 - w], ((0, 0), (w - 1, 0), (0, 0)))
        count = jnp.minimum(pos, w).astype(jnp.float32)[None, :, None]
        d = (upper - lower) / count - uf[:, :, sl]
        outs.append(jnp.einsum('bsc,cd->bsd', d.astype(u.dtype), w_grp[g]))
    y = (jnp.concatenate(outs, axis=-1) + b_grp) * scale
    y = y * jax.nn.silu(z)
    return jnp.einsum('bse,ed->bsd', y, w_out)


def setup_inputs(seed: int = 0) -> dict:
    key = jax.random.key(seed)
    ks = jax.random.split(key, 16)
    D, EA, EB = D_MODEL, CONV_INNER, POOL_INNER
    nrm = jax.random.normal
    f32 = jnp.float32
    return {
        "x": nrm(ks[0], (BATCH, SEQ, D), f32),
        "norm_g": 1.0 + 0.05 * nrm(ks[1], (DEPTH, D), f32),
        "final_g": 1.0 + 0.05 * nrm(ks[2], (D,), f32),
        "conv_w_in": nrm(ks[3], (N_A, D, 3 * EA), f32) * D ** -0.5,
        "conv_dw": nrm(ks[4], (N_A, CONV_WIDTH, EA), f32) * CONV_WIDTH ** -0.5,
        "conv_dw_b": 0.02 * nrm(ks[5], (N_A, EA), f32),
        "conv_ln_g": 1.0 + 0.05 * nrm(ks[6], (N_A, EA), f32),
        "conv_ln_b": 0.02 * nrm(ks[7], (N_A, EA), f32),
        "conv_w_out": nrm(ks[8], (N_A, EA, D), f32) * EA ** -0.5,
        "pool_w_in": nrm(ks[9], (N_B, D, 2 * EB), f32) * D ** -0.5,
        "pool_w_grp": nrm(ks[10], (N_B, POOL_GROUPS, POOL_GC, POOL_GC), f32) * POOL_GC ** -0.5,
        "pool_b_grp": 0.02 * nrm(ks[11], (N_B, EB), f32),
        "pool_scale": 1.0 + 0.1 * nrm(ks[12], (N_B, EB), f32),
        "pool_w_out": nrm(ks[13], (N_B, EB, D), f32) * EB ** -0.5,
    }


def reference(x, norm_g, final_g, conv_w_in, conv_dw, conv_dw_b, conv_ln_g,
              conv_ln_b, conv_w_out, pool_w_in, pool_w_grp, pool_b_grp,
              pool_scale, pool_w_out):
    h = x
    for i in range(DEPTH):
        hn = rmsnorm(h, norm_g[i])
        j = i // N_MIXERS
        if i % N_MIXERS == 0:
            y = conformer_conv_branch(hn, conv_w_in[j], conv_dw[j], conv_dw_b[j],
                                      conv_ln_g[j], conv_ln_b[j], conv_w_out[j])
        else:
            y = multiscale_pool_branch(hn, pool_w_in[j], pool_w_grp[j], pool_b_grp[j],
                                       pool_scale[j], pool_w_out[j])
        h = h + y
    return rmsnorm(h, final_g)
```

```python
import numpy as np
from collections import defaultdict
import concourse.bass as bass
import concourse.mybir as mybir
from concourse.bass_utils import run_bass_kernel_spmd

F32 = mybir.dt.float32
BF16 = mybir.dt.bfloat16
ALU = mybir.AluOpType
AF = mybir.ActivationFunctionType

NCORES = 8
D = 1024
NQ = 8
SEQ = 16384
TOK_PER_CORE = 4096
NBLK = 33
H = 32
RMS_EPS = 1e-6
LN_EPS = 1e-5
WINDOWS = (2, 4, 8, 16)

TILES = [list(range(0, 7)), list(range(7, 14)), list(range(14, 21)),
         list(range(21, 27)), list(range(27, 33))]
XSLOTS = 7
DBG_LAYERS = 4
NTMP = 4
NWD = 3
NRB = 4
_LAST = {}


def _groups(tile_blocks):
    if tile_blocks[0] == 0:
        gs = [[0]]
        rest = tile_blocks[1:]
    else:
        gs = []
        rest = tile_blocks
    i = 0
    while i < len(rest):
        gs.append(rest[i:i + 2])
        i += 2
    return gs


P_GAIN = 0
P_DWB = 40
P_LNG = 56
P_LNB = 72
P_BG = 88
P_SC = 104
P_DW = 120
P_ID = 632
P_MASK = 696
NPRM = 824

WA_COLS = 8 * 3072 + 8 * 1024
WA_OUT = 8 * 3072
WB_COLS = 8 * 2048 + 4 * 2 * 256 + 8 * 1024
WB_GRP = 8 * 2048
WB_OUT = WB_GRP + 2048


class _Res:
    __slots__ = ("w", "r")

    def __init__(self):
        self.w = None
        self.r = {}


class _DSem:
    def __init__(self, sem):
        self.sem = sem
        self.cnt = 0


class _Q:
    def __init__(self, sem, res, inorder=False):
        self.sem = sem
        self.inorder = inorder
        self.cnt = 0
        self.waited = {}
        self.res = res
        self.prog = []

    def _wait(self, tok):
        if tok is None:
            return
        s, v = tok
        if self.inorder and s is self.sem:
            return
        k = id(s)
        if self.waited.get(k, 0) >= v:
            return
        self.prog.append(("w", s, v))
        self.waited[k] = v

    def deps(self, reads, writes):
        need = {}

        def add(tok):
            if tok is None:
                return
            k = id(tok[0])
            if k not in need or need[k][1] < tok[1]:
                need[k] = tok
        for r in reads:
            add(self.res[r].w)
        for w in writes:
            rr = self.res[w]
            if rr.r:
                for tok in rr.r.values():
                    add(tok)
            else:
                add(rr.w)
        for tok in need.values():
            self._wait(tok)

    def commit(self, tok, reads, writes):
        k = id(tok[0])
        for r in reads:
            rr = self.res[r].r
            if k not in rr or rr[k][1] < tok[1]:
                rr[k] = tok
        for w in writes:
            rr = self.res[w]
            rr.w = tok
            rr.r = {}

    def op(self, fn, reads=(), writes=()):
        self.deps(reads, writes)
        self.cnt += 1
        self.prog.append(("i", fn, self.sem, 1))
        self.commit((self.sem, self.cnt), reads, writes)

    def mm_group(self, mms, reads=(), writes=(), label=""):
        self.deps(reads, writes)
        for kw in mms[:-1]:
            self.prog.append(("m", kw, None, 0, label))
        self.cnt += 1
        self.prog.append(("m", mms[-1], self.sem, 1, label))
        self.commit((self.sem, self.cnt), reads, writes)

    def dma_group(self, dsem, fns, reads=(), writes=()):
        self.deps(reads, writes)
        for fn in fns:
            dsem.cnt += 16
            self.prog.append(("i", fn, dsem.sem, 16))
        self.commit((dsem.sem, dsem.cnt), reads, writes)

    def replay(self, q):
        for it in self.prog:
            if it[0] == "w":
                q.wait_ge(it[1], it[2])
            elif it[0] == "i":
                ins = it[1](q)
                if it[2] is not None:
                    ins.then_inc(it[2], it[3])
            else:
                ins = q.matmul(**it[1])
                if it[2] is not None:
                    ins.then_inc(it[2], it[3])


def build_program():
    nc = bass.Bass("TRN2", target_bir_lowering=False)
    x_d = nc.dram_tensor("x", [128, NBLK, NQ, 128], F32, kind="ExternalInput").ap()
    wa_d = nc.dram_tensor("wa", [2, 128, WA_COLS], F32, kind="ExternalInput").ap()
    wb_d = nc.dram_tensor("wb", [2, 128, WB_COLS], F32, kind="ExternalInput").ap()
    prm_d = nc.dram_tensor("prm", [128, NPRM], F32, kind="ExternalInput").ap()
    pm_d = nc.dram_tensor("pm", [128, 12 * 128], F32, kind="ExternalInput").ap()
    out_d = nc.dram_tensor("out", [128, NBLK - 1, NQ, 128], F32, kind="ExternalOutput").ap()
    wds_d = nc.dram_tensor("wds", [16, 128, 32 * 64], BF16, kind="Internal").ap()

    from contextlib import ExitStack
    with ExitStack() as es:
        def sb(name, shape, dt):
            return es.enter_context(nc.sbuf_tensor(name, shape, dt))

        def sem(name):
            return es.enter_context(nc.semaphore(name))

        x_sb = sb("x_sb", [128, XSLOTS, NQ, 128], F32)
        WA = sb("WA", [128, WA_COLS], BF16)
        WB = sb("WB", [128, WB_COLS], BF16)
        prm = sb("prm_sb", [128, NPRM], F32)
        PM = sb("PM", [128, 12, 128], BF16)
        ones = sb("ones", [128, 128], BF16)
        epsr = sb("epsr", [128, 1], F32)
        zero1 = sb("zero1", [128, 1], F32)
        bs = sb("bs", [128, 16], F32)
        hist = [sb(f"hist{j}", [128, NQ, H], BF16) for j in range(2)]
        prevT = [sb(f"prevT{j}", [128, D], BF16) for j in range(2)]
        hnb = sb("hnb", [128, NQ, 256], BF16)
        RW = H + 256 + 16
        RA = sb("RA", [128, NQ, 2, RW], BF16)
        szb = [sb(f"szb{i}", [128, NQ, 256], BF16) for i in range(2)]
        vb = sb("vb", [128, NQ, 256], BF16)
        gtb = sb("gtb", [128, NQ, 256], BF16)
        uT = sb("uT", [128, 2, D], BF16)
        Wd = [sb(f"Wd{i}", [128, 32, 64], BF16) for i in range(NWD)]
        Wd.append(uT[:].rearrange("p a (b c) -> p (a b) c", c=64))
        UT_ALIAS = ["uT0_0", "uT0_1", "uT1_0", "uT1_1"]
        st_a = sb("st_a", [128, 256], F32)
        st_b = sb("st_b", [128, 256], F32)
        st_c = sb("st_c", [128, 256], F32)
        tmp = [sb(f"tmp{i}", [128, 256], F32) for i in range(NTMP - 1)]
        tmp.append(prm[:, P_DW:P_DW + 256])
        ps = [es.enter_context(nc.psum_tensor(f"ps{i}", [128, 512], F32)) for i in range(8)]

        s_pe, s_act, s_dve, s_pool = sem("s_pe"), sem("s_act"), sem("s_dve"), sem("s_pool")
        d_const = _DSem(sem("d_const"))
        d_pm = _DSem(sem("d_pm"))
        d_wa = _DSem(sem("d_wa"))
        d_wb = _DSem(sem("d_wb"))
        d_x = [_DSem(sem(f"d_x{i}")) for i in range(XSLOTS)]
        d_out = [_DSem(sem(f"d_out{i}")) for i in range(4)]
        d_r = [_DSem(sem(f"d_r{i}")) for i in range(NQ)]
        d_prev = [_DSem(sem(f"d_prev{i}")) for i in range(2)]
        d_wdst = [_DSem(sem(f"d_wdst{i}")) for i in range(NWD)]
        d_wdld = [_DSem(sem(f"d_wdld{i}")) for i in range(NWD + 1)]

        res = defaultdict(_Res)
        PE = _Q(s_pe, res, inorder=True)
        ACT = _Q(s_act, res)
        DVE = _Q(s_dve, res)
        POOL = _Q(s_pool, res)
        SP = _Q(None, res)

        cnt = {"tmp": 0, "r": 0, "wd": 0}
        preload = {}

        def slot(b, h, N):
            return ps[b][:, h * 256:h * 256 + N]

        def pcol(c):
            return prm[:, c:c + 1]

        SP.dma_group(d_const, [lambda q: q.dma_start(out=prm[:], in_=prm_d[:])], writes=["prm"])
        POOL.dma_group(d_pm, [lambda q: q.dma_start(out=PM[:].rearrange("p a b -> p (a b)"), in_=pm_d[:])],
                       writes=["PM"])
        POOL.op(lambda q: q.memset(ones[:], 1.0 / D), writes=["ones"])
        POOL.op(lambda q: q.memset(epsr[:], RMS_EPS), writes=["eps"])
        POOL.op(lambda q: q.memset(zero1[:], 0.0), writes=["eps"])
        for j in range(2):
            POOL.op(lambda q, j=j: q.memset(hist[j][:], 0.0), writes=[f"hist{j}"])
            POOL.op(lambda q, j=j: q.memset(prevT[j][:], 0.0), writes=[f"prevT{j}"])
        POOL.op(lambda q: q.memset(RA[:], 0.0), writes=[f"uo{k}" for k in range(NQ)] + [f"uc{k}" for k in range(NQ)])
        DVE.op(lambda q: q.tensor_tensor(out=bs[:], in0=prm[:, P_BG:P_BG + 16], in1=prm[:, P_SC:P_SC + 16],
                                         op=ALU.mult), reads=["prm"], writes=["bs"])

        PIECE = 2048
        WA_RES = [f"WA{i}" for i in range(WA_COLS // PIECE)]
        WB_RES = [f"WB{i}" for i in range(WB_COLS // PIECE)]
        pending = []

        def queue_weights(kind, j):
            if kind == "A":
                for i in range(WA_COLS // PIECE):
                    pending.append(("A", j, i))
            else:
                for i in range(WB_COLS // PIECE):
                    pending.append(("B", j, i))

        def pump(n=1):
            for _ in range(n):
                if not pending:
                    return
                kind, j, i = pending.pop(0)
                a, b = i * PIECE, (i + 1) * PIECE
                if kind == "A":
                    POOL.dma_group(d_wa, [lambda q, a=a, b=b, j=j: q.dma_start(out=WA[:, a:b], in_=wa_d[j, :, a:b])],
                                   writes=[f"WA{i}"])
                else:
                    POOL.dma_group(d_wb, [lambda q, a=a, b=b, j=j: q.dma_start(out=WB[:, a:b], in_=wb_d[j, :, a:b])],
                                   writes=[f"WB{i}"])

        def xview(k0, nb):
            return x_sb[:, k0:k0 + nb, :, :]

        def rms_stage(L, k0, nb, xres):
            N = 128 * nb
            ACT.op(lambda q: q.activation(out=hnb[:, :, 0:N].rearrange("p q (b t) -> p b q t", b=nb),
                                          in_=xview(k0, nb), func=AF.Square),
                   reads=xres, writes=["hn"])
            PE.mm_group([dict(out=slot(6, 0, N), lhsT=ones[:], rhs=hnb[:, q, 0:N],
                              start=(q == 0), stop=(q == NQ - 1)) for q in range(NQ)],
                        reads=["hn", "ones"], writes=["ps6"], label="rms_ss")
            ACT.op(lambda q: q.activation(out=st_a[:, 0:N], in_=slot(6, 0, N), func=AF.Sqrt,
                                          bias=epsr[:, 0:1], scale=1.0),
                   reads=["ps6", "eps"], writes=["st_a"])

        def hn_stage(L, k0, nb, xres):
            N = 128 * nb
            DVE.op(lambda q: q.reciprocal(out=st_a[:, 0:N], in_=st_a[:, 0:N]), reads=["st_a"], writes=["st_a"])
            for qq in range(NQ):
                DVE.op(lambda q, qq=qq: q.scalar_tensor_tensor(
                    out=hnb[:, qq, 0:N].rearrange("p (b t) -> p b t", b=nb),
                    in0=x_sb[:, k0:k0 + nb, qq, :], scalar=pcol(P_GAIN + L * 8 + qq),
                    in1=st_a[:, 0:N].rearrange("p (b t) -> p b t", b=nb),
                    op0=ALU.mult, op1=ALU.mult),
                    reads=xres + ["st_a", "prm"], writes=["hn"])

        def outproj_stage(Wt, wres, base, k0, nb, xres, mask_halo, mid=None):
            N = 128 * nb
            for dq in range(NQ):
                if dq == 4 and mid is not None:
                    mid()
                b, h = (7 if dq % 2 == 0 else 5), 0
                PE.mm_group([dict(out=slot(b, h, N), lhsT=Wt[:, base + eq * 1024 + dq * 128: base + eq * 1024 + dq * 128 + 128],
                                  rhs=gtb[:, eq, 0:N], start=(eq == 0), stop=(eq == NQ - 1)) for eq in range(NQ)],
                            reads=wres + [f"gt{e}" for e in range(NQ)], writes=[f"ps{b}"], label=f"outproj{dq}")
                DVE.op(lambda q, dq=dq, b=b, h=h: q.tensor_tensor(
                    out=x_sb[:, k0:k0 + nb, dq, :], in0=x_sb[:, k0:k0 + nb, dq, :],
                    in1=slot(b, h, N).rearrange("p (b t) -> p b t", b=nb), op=ALU.add),
                    reads=[f"ps{b}"] + xres, writes=xres)
            if mask_halo:
                DVE.op(lambda q: q.tensor_tensor(
                    out=x_sb[:, 0, :, :], in0=x_sb[:, 0, :, :],
                    in1=prm[:, P_MASK:P_MASK + 128].unsqueeze(1).to_broadcast([128, NQ, 128]), op=ALU.mult),
                    reads=["x0", "prm"], writes=["x0"])

        class Item:
            pass

        def xres_of(it):
            return [f"x{s}" for s in range(it.k0, it.k0 + it.nb)]

        def phase1a(it):
            rms_stage(it.L, it.k0, it.nb, xres_of(it))

        def phase1b(it):
            xres = xres_of(it)
            k0, nb, N = it.k0, it.nb, 128 * it.nb
            if it.kind != "final":
                hn_stage(it.L, k0, nb, xres)
                return
            DVE.op(lambda q: q.reciprocal(out=st_a[:, 0:N], in_=st_a[:, 0:N]), reads=["st_a"], writes=["st_a"])
            fall = []
            for hq in range(2):
                fres = [f"xF{k0}_{hq}"]
                fall += fres
                DVE.op(lambda q, hq=hq: q.tensor_tensor(
                    out=x_sb[:, k0:k0 + nb, 4 * hq:4 * hq + 4, :], in0=x_sb[:, k0:k0 + nb, 4 * hq:4 * hq + 4, :],
                    in1=st_a[:, 0:N].rearrange("p (b t) -> p b t", b=nb).unsqueeze(2).to_broadcast([128, nb, 4, 128]),
                    op=ALU.mult),
                    reads=xres + ["st_a"], writes=xres + fres)
                for qq in range(4 * hq, 4 * hq + 4):
                    ACT.op(lambda q, qq=qq: q.activation(out=x_sb[:, k0:k0 + nb, qq, :], in_=x_sb[:, k0:k0 + nb, qq, :],
                                                        func=AF.Copy, scale=pcol(P_GAIN + 32 + qq)),
                           reads=fres + ["prm"], writes=fres)
            SP.dma_group(d_out[it.gidx], [lambda q: q.dma_start(out=out_d[:, it.blk0 - 1:it.blk0 - 1 + nb, :, :],
                                                                in_=x_sb[:, k0:k0 + nb, :, :])],
                         reads=xres + fall, writes=[])
            if it.t + 1 < len(TILES):
                slots = list(range(k0, k0 + nb))
                if it.t == 0 and it.gidx == 1:
                    slots = [0] + slots
                for s_ in slots:
                    if s_ < len(TILES[it.t + 1]):
                        load_x_slot(it.t + 1, s_)

        def phase2(it, fill):
            nunits = 8 if it.kind == "conv" else (2 * it.nb + 8)
            nfill = len(fill)
            state = {"u": 0}

            def pop():
                pump(2)
                state["u"] += 1
                target = nfill if state["u"] >= nunits - 2 else (nfill * state["u"] + nunits - 3) // max(1, nunits - 2)
                while fill and nfill - len(fill) < target:
                    fill.pop(0)()
            if it.kind == "final":
                while fill:
                    pop()
                return
            j = it.L // 2
            k0, nb, N = it.k0, it.nb, 128 * it.nb
            sz = szb[it.par]
            szr = f"sz{it.par}_"
            if it.kind == "conv":
                ures = [f"uo{q}" for q in range(NQ)]
                POOL.op(lambda q: q.tensor_copy(out=RA[0:64, :, 0, 0:H], in_=hist[j][0:64, :, :]),
                        reads=[f"hist{j}"], writes=ures)
                POOL.op(lambda q: q.tensor_copy(out=RA[64:128, :, 1, 16:16 + H], in_=hist[j][64:128, :, :]),
                        reads=[f"hist{j}"], writes=ures)
                for qq in range(NQ):
                    abk = (0, 1, 4)[qq % 3]
                    zbk = (2, 3, 5)[qq % 3]
                    for part, bank, half in ((0, abk, 0), (1, abk, 1), (2, zbk, 0)):
                        c0 = part * 1024 + qq * 128
                        PE.mm_group([dict(out=slot(bank, half, N), lhsT=WA[:, kq * 3072 + c0: kq * 3072 + c0 + 128],
                                          rhs=hnb[:, kq, 0:N], start=(kq == 0), stop=(kq == NQ - 1))
                                     for kq in range(NQ)], reads=["hn"] + WA_RES, writes=[f"ps{bank}"], label=f"c.inproj{qq}.{part}")
                    ti = cnt["tmp"] % NTMP
                    cnt["tmp"] += 1
                    ACT.op(lambda q, ti=ti, abk=abk: q.activation(out=tmp[ti][:, 0:N], in_=slot(abk, 1, N),
                                                                 func=AF.Tanh, scale=0.5),
                           reads=[f"ps{abk}"], writes=[f"tmp{ti}"])
                    DVE.op(lambda q, ti=ti, abk=abk, qq=qq: q.scalar_tensor_tensor(
                        out=RA[0:64, qq, 0, H:H + N], in0=tmp[ti][0:64, 0:N], scalar=1.0, in1=slot(abk, 0, N)[0:64, :],
                        op0=ALU.add, op1=ALU.mult),
                        reads=[f"tmp{ti}", f"ps{abk}"], writes=[f"uo{qq}"])
                    DVE.op(lambda q, ti=ti, abk=abk, qq=qq: q.scalar_tensor_tensor(
                        out=RA[64:128, qq, 1, H + 16:H + 16 + N], in0=tmp[ti][64:128, 0:N], scalar=1.0,
                        in1=slot(abk, 0, N)[64:128, :], op0=ALU.add, op1=ALU.mult),
                        reads=[f"tmp{ti}", f"ps{abk}"], writes=[f"uo{qq}"])
                    SP.dma_group(d_r[qq], [
                        lambda q, qq=qq: q.dma_start(out=RA[0:64, qq, 1, 0:H + N], in_=RA[64:128, qq, 1, 16:16 + H + N]),
                        lambda q, qq=qq: q.dma_start(out=RA[64:128, qq, 0, 16:16 + H + N], in_=RA[0:64, qq, 0, 0:H + N])],
                        reads=[f"uo{qq}"], writes=[f"uc{qq}"])
                    ACT.op(lambda q, zbk=zbk, qq=qq: q.activation(out=sz[:, qq, 0:N], in_=slot(zbk, 0, N), func=AF.Silu),
                           reads=[f"ps{zbk}"], writes=[szr + str(qq)])
                    pop()
                POOL.op(lambda q: q.tensor_copy(out=hist[j][0:64, :, :], in_=RA[0:64, :, 0, N:N + H]),
                        reads=ures, writes=[f"hist{j}"])
                POOL.op(lambda q: q.tensor_copy(out=hist[j][64:128, :, :], in_=RA[64:128, :, 1, N + 16:N + 16 + H]),
                        reads=ures, writes=[f"hist{j}"])
            else:
                for bb in range(nb):
                    for hh in range(2):
                        bank = (0, 1, 4)[(bb * 2 + hh) % 3]
                        PE.mm_group([dict(out=ps[bank][:, :], lhsT=hnb[:, kq, bb * 128:(bb + 1) * 128],
                                          rhs=WB[:, kq * 2048 + hh * 512: kq * 2048 + hh * 512 + 512],
                                          start=(kq == 0), stop=(kq == NQ - 1)) for kq in range(NQ)],
                                    reads=["hn"] + WB_RES, writes=[f"ps{bank}"], label=f"p.uT{bb}{hh}")
                        ACT.op(lambda q, bb=bb, hh=hh, bank=bank: q.activation(
                            out=uT[:, bb, hh * 512:(hh + 1) * 512], in_=ps[bank][:, :], func=AF.Copy),
                            reads=[f"ps{bank}"], writes=[f"uT{bb}_{hh}"])
                        pop()
                for qq in range(NQ):
                    zbk = (2, 3, 5)[qq % 3]
                    c0 = 1024 + qq * 128
                    PE.mm_group([dict(out=slot(zbk, 0, N), lhsT=WB[:, kq * 2048 + c0: kq * 2048 + c0 + 128],
                                      rhs=hnb[:, kq, 0:N], start=(kq == 0), stop=(kq == NQ - 1)) for kq in range(NQ)],
                                reads=["hn"] + WB_RES, writes=[f"ps{zbk}"], label=f"p.z{qq}")
                    ACT.op(lambda q, zbk=zbk, qq=qq: q.activation(out=sz[:, qq, 0:N], in_=slot(zbk, 0, N), func=AF.Silu),
                           reads=[f"ps{zbk}"], writes=[szr + str(qq)])
                    pop()
            while fill:
                pop()

        def phase3(it, mid):
            if it.kind == "final":
                mid()
                return
            if it.kind == "conv":
                mid()
            j = it.L // 2
            k0, nb, N = it.k0, it.nb, 128 * it.nb
            sz = szb[it.par]
            szr = f"sz{it.par}_"
            if it.kind == "conv":
                wsel = preload.pop(id(it), {})

                def wd_load(qq, jj=j, sel=None):
                    sel = wsel if sel is None else sel
                    wp = cnt["wd"] % (NWD + 1)
                    cnt["wd"] += 1
                    wdres = [f"Wd{wp}"] + (UT_ALIAS if wp == NWD else [])
                    sel[qq] = (wp, wdres)
                    widx = jj * 8 + qq
                    SP.dma_group(d_wdld[wp], [lambda q, wp=wp, widx=widx, hf=hf: q.dma_start(
                        out=Wd[wp][:, 16 * hf:16 * hf + 16, :].rearrange("p a b -> p (a b)"),
                        in_=wds_d[widx][:, 1024 * hf:1024 * hf + 1024]) for hf in range(2)],
                        reads=[f"wds{widx}"], writes=wdres)

                for qq in range(4):
                    if qq not in wsel:
                        wd_load(qq)
                for qq in range(NQ):
                    par = qq % 2
                    wp, wdres = wsel[qq]
                    mms = []
                    for kk in range(16):
                        for m in range(2):
                            mms.append(dict(out=slot(4 + par, 0, N)[64 * m:64 * m + 64, :], lhsT=Wd[wp][:, m * 16 + kk, :],
                                            rhs=RA[:, qq, m, H - kk:H - kk + N], start=(kk == 0), stop=(kk == 15),
                                            skip_group_check=True, tile_position=(0, 64 * m)))
                    PE.mm_group(mms, reads=[f"uo{qq}", f"uc{qq}"] + wdres, writes=[f"ps{4 + par}"], label=f"c.conv{qq}")
                    if qq + 4 < NQ:
                        wd_load(qq + 4)
                    else:
                        nci = it.next_conv
                        if nci is not None and (nci is it.next_item or qq < 7):
                            wd_load(qq - 4, jj=nci.L // 2, sel=preload.setdefault(id(nci), {}))
                    ACT.op(lambda q, par=par, qq=qq: q.activation(out=vb[:, qq, 0:N], in_=slot(4 + par, 0, N), func=AF.Identity,
                                                                 bias=pcol(P_DWB + j * 8 + qq), scale=1.0),
                           reads=[f"ps{4 + par}", "prm"], writes=[f"v{qq}"])
                    ACT.op(lambda q, par=par, qq=qq: q.activation(out=gtb[:, qq, 0:N], in_=slot(4 + par, 0, N), func=AF.Square,
                                                                 bias=pcol(P_DWB + j * 8 + qq), scale=1.0),
                           reads=[f"ps{4 + par}", "prm"], writes=[f"gt{qq}"])
                vres = [f"v{q}" for q in range(NQ)]
                gres = [f"gt{q}" for q in range(NQ)]
                PE.mm_group([dict(out=slot(6, 0, N), lhsT=ones[:], rhs=vb[:, q, 0:N], start=(q == 0), stop=(q == NQ - 1))
                             for q in range(NQ)], reads=vres + ["ones"], writes=["ps6"], label="c.ln_s1")
                PE.mm_group([dict(out=slot(6, 1, N), lhsT=ones[:], rhs=gtb[:, q, 0:N], start=(q == 0), stop=(q == NQ - 1))
                             for q in range(NQ)], reads=gres + ["ones"], writes=["ps6"], label="c.ln_s2")
            else:
                for qq in range(NQ):
                    par = qq % 2
                    dbk = (4, 5, 7, 6)[qq % 4]
                    gi = qq // 2
                    hh = qq // 4
                    mms = []
                    rd = ["PM"]
                    for bb in range(nb):
                        kind = 2 if (it.first_blk and bb == 0) else 0
                        o = slot(dbk, 0, N)[:, bb * 128:(bb + 1) * 128]
                        mms.append(dict(out=o, lhsT=uT[:, bb, qq * 128:(qq + 1) * 128], rhs=PM[:, kind * 4 + gi, :],
                                        start=True, stop=False))
                        if bb == 0:
                            pl = prevT[j][:, qq * 128:(qq + 1) * 128]
                            rd.append(f"prevT{j}")
                        else:
                            pl = uT[:, bb - 1, qq * 128:(qq + 1) * 128]
                        mms.append(dict(out=o, lhsT=pl, rhs=PM[:, 4 + gi, :], start=False, stop=True))
                        rd.append(f"uT{bb}_{hh}")
                    PE.mm_group(mms, reads=rd, writes=[f"ps{dbk}"], label=f"p.pool{qq}")
                    ACT.op(lambda q, dbk=dbk, qq=qq: q.activation(out=vb[:, qq, 0:N], in_=slot(dbk, 0, N), func=AF.Copy),
                           reads=[f"ps{dbk}"], writes=[f"v{qq}"])
                mid()
                SP.dma_group(d_prev[j], [lambda q: q.dma_start(out=prevT[j][:, :], in_=uT[:, nb - 1, :])],
                             reads=[f"uT{nb - 1}_0", f"uT{nb - 1}_1"], writes=[f"prevT{j}"])
                for cq in range(NQ):
                    par = cq % 2
                    gi = cq // 2
                    mmi = cq % 2
                    base = WB_GRP + gi * 512
                    yb = cq % 4
                    PE.mm_group([dict(out=slot(yb, 0, N), lhsT=WB[:, base + kk * 256 + mmi * 128: base + kk * 256 + mmi * 128 + 128],
                                      rhs=vb[:, 2 * gi + kk, 0:N], start=(kk == 0), stop=(kk == 1)) for kk in range(2)],
                                reads=WB_RES + [f"v{2 * gi}", f"v{2 * gi + 1}"], writes=[f"ps{yb}"], label=f"p.grp{cq}")
                    ACT.op(lambda q, yb=yb, cq=cq: q.activation(
                        out=gtb[:, cq, 0:N], in_=slot(yb, 0, N), func=AF.Identity,
                        bias=bs[:, j * 8 + cq:j * 8 + cq + 1], scale=pcol(P_SC + j * 8 + cq)),
                        reads=[f"ps{yb}", "bs", "prm"], writes=[f"gt{cq}"])

        def phase4_steps(it):
            if it.kind == "final":
                return []
            j = it.L // 2
            N = 128 * it.nb
            sz = szb[it.par]
            szr = f"sz{it.par}_"
            if it.kind == "pool":
                def mkp(h):
                    def step():
                        DVE.op(lambda q: q.tensor_tensor(out=gtb[:, 4 * h:4 * h + 4, 0:N], in0=gtb[:, 4 * h:4 * h + 4, 0:N],
                                                         in1=sz[:, 4 * h:4 * h + 4, 0:N], op=ALU.mult),
                               reads=[f"gt{c}" for c in range(4 * h, 4 * h + 4)] + [szr + str(c) for c in range(4 * h, 4 * h + 4)],
                               writes=[f"gt{c}" for c in range(4 * h, 4 * h + 4)])
                    return step
                return [mkp(0), mkp(1)]

            def chain():
                ACT.op(lambda q: q.activation(out=st_b[:, 0:N], in_=slot(6, 0, N), func=AF.Square),
                       reads=["ps6"], writes=["st_b"])
                DVE.op(lambda q: q.scalar_tensor_tensor(out=st_c[:, 0:N], in0=slot(6, 1, N), scalar=LN_EPS,
                                                        in1=st_b[:, 0:N], op0=ALU.add, op1=ALU.subtract),
                       reads=["ps6", "st_b"], writes=["st_c"])
                ACT.op(lambda q: q.activation(out=st_c[:, 0:N], in_=st_c[:, 0:N], func=AF.Sqrt,
                                              bias=zero1[:, 0:1], scale=1.0),
                       reads=["st_c", "eps"], writes=["st_c"])
                DVE.op(lambda q: q.reciprocal(out=st_c[:, 0:N], in_=st_c[:, 0:N]), reads=["st_c"], writes=["st_c"])
                DVE.op(lambda q: q.scalar_tensor_tensor(out=st_b[:, 0:N], in0=slot(6, 0, N), scalar=-1.0,
                                                        in1=st_c[:, 0:N], op0=ALU.mult, op1=ALU.mult),
                       reads=["ps6", "st_c"], writes=["st_b"])

            tmps = {}

            def A(qq):
                ti = cnt["tmp"] % NTMP
                cnt["tmp"] += 1
                tmps[qq] = ti
                DVE.op(lambda q: q.tensor_tensor(out=tmp[ti][:, 0:N], in0=vb[:, qq, 0:N],
                                                 in1=st_c[:, 0:N], op=ALU.mult),
                       reads=[f"v{qq}", "st_c"], writes=[f"tmp{ti}"])
                DVE.op(lambda q: q.tensor_tensor(out=tmp[ti][:, 0:N], in0=tmp[ti][:, 0:N],
                                                 in1=st_b[:, 0:N], op=ALU.add),
                       reads=[f"tmp{ti}", "st_b"], writes=[f"tmp{ti}"])

            def B(qq):
                ti = tmps[qq]
                ACT.op(lambda q: q.activation(out=gtb[:, qq, 0:N], in_=tmp[ti][:, 0:N], func=AF.Silu,
                                              bias=pcol(P_LNB + j * 8 + qq), scale=pcol(P_LNG + j * 8 + qq)),
                       reads=[f"tmp{ti}", "prm"], writes=[f"gt{qq}"])

            def C(qq):
                DVE.op(lambda q: q.tensor_tensor(out=gtb[:, qq, 0:N], in0=gtb[:, qq, 0:N],
                                                 in1=sz[:, qq, 0:N], op=ALU.mult),
                       reads=[f"gt{qq}", szr + str(qq)], writes=[f"gt{qq}"])

            def mk(e):
                def step():
                    if 0 <= e - 2 < NQ:
                        C(e - 2)
                    if 0 <= e - 1 < NQ:
                        B(e - 1)
                    if 0 <= e < NQ:
                        A(e)
                return step
            return [chain] + [mk(e) for e in range(NQ + 2)]

        def phase5(it, mid=None):
            if it.kind == "final":
                if mid is not None:
                    mid()
                return
            if it.kind == "conv":
                outproj_stage(WA, WA_RES, WA_OUT, it.k0, it.nb, xres_of(it), it.mask_halo, mid)
            else:
                outproj_stage(WB, WB_RES, WB_OUT, it.k0, it.nb, xres_of(it), it.mask_halo, mid)

        items = []
        nlay = DBG_LAYERS
        for t, blocks in enumerate(TILES):
            groups = _groups(blocks)
            for L in list(range(nlay)) + [4]:
                for gi, g in enumerate(groups):
                    it = Item()
                    it.kind = "final" if L == 4 else ("conv" if L % 2 == 0 else "pool")
                    it.L, it.t, it.gidx = L, t, gi
                    it.k0, it.nb, it.blk0 = g[0] - blocks[0], len(g), g[0]
                    it.mask_halo = (g[0] == 0)
                    it.first_blk = (g[0] == 1)
                    it.par = len(items) % 2
                    it.last_of_step = (gi == len(groups) - 1)
                    it.first_of_tile = (L == 0 and gi == 0)
                    if it.kind == "final" and g[0] == 0:
                        continue
                    items.append(it)

        last_conv = None
        nxt_item = None
        for it_ in reversed(items):
            it_.next_conv = last_conv
            it_.next_item = nxt_item
            nxt_item = it_
            if it_.kind == "conv":
                last_conv = it_

        def load_x_slot(t, s_):
            blk = TILES[t][s_]
            SP.dma_group(d_x[s_], [lambda q: q.dma_start(out=x_sb[:, s_, :, :], in_=x_d[:, blk, :, :])],
                         writes=[f"x{s_}"])

        def load_x(t):
            for s_ in range(len(TILES[t])):
                load_x_slot(t, s_)

        step_list = [(t, L) for t in range(len(TILES)) for L in range(nlay)]

        def after_phase5(it):
            if it.kind == "final" or not it.last_of_step:
                return
            si = step_list.index((it.t, it.L))
            pump(1000)
            if si + 2 < len(step_list):
                t2, L2 = step_list[si + 2]
                queue_weights("A" if L2 % 2 == 0 else "B", L2 // 2)

        queue_weights("A", 0)
        pump(1000)
        for idx in range(16):
            wp = idx % NWD
            dwc = P_DW + idx * 32
            POOL.op(lambda q, wp=wp, dwc=dwc: q.tensor_tensor(
                out=Wd[wp][:], in0=prm[:, P_ID:P_ID + 64].unsqueeze(1).to_broadcast([128, 32, 64]),
                in1=prm[:, dwc:dwc + 32].unsqueeze(2).to_broadcast([128, 32, 64]), op=ALU.mult),
                reads=["prm"], writes=[f"Wd{wp}"])
            SP.dma_group(d_wdst[wp], [lambda q, wp=wp, idx=idx: q.dma_start(
                out=wds_d[idx], in_=Wd[wp][:].rearrange("p a b -> p (a b)"))],
                reads=[f"Wd{wp}"], writes=[f"wds{idx}"])

        res[f"tmp{NTMP - 1}"].r[id(s_pool)] = (s_pool, POOL.cnt)
        if nlay > 1:
            queue_weights("B", 0)
        pump(1000)
        load_x(0)
        phase1a(items[0])
        phase1b(items[0])
        phase2(items[0], [])
        if len(items) > 1:
            phase1a(items[1])
            phase1b(items[1])
        for i, it in enumerate(items):
            nxt = items[i + 1] if i + 1 < len(items) else None
            nxt2 = items[i + 2] if i + 2 < len(items) else None
            phase3(it, lambda: None)
            fill = phase4_steps(it)
            if nxt is not None:
                phase2(nxt, fill)
            while fill:
                fill.pop(0)()
            phase5(it, (lambda: phase1a(nxt2)) if nxt2 is not None else None)
            if nxt2 is not None:
                phase1b(nxt2)
            after_phase5(it)
        for ds in d_out + d_prev + d_r + d_wdld + d_wdst + d_x + [d_wa, d_wb, d_const, d_pm]:
            if ds.cnt:
                SP.prog.append(("w", ds.sem, ds.cnt))

        _LAST["PE"] = PE.prog
        with nc.Block() as block:
            @block.tensor
            def _(q):
                PE.replay(q)

            @block.scalar
            def _(q):
                ACT.replay(q)

            @block.vector
            def _(q):
                DVE.replay(q)

            @block.gpsimd
            def _(q):
                POOL.replay(q)

            @block.sync
            def _(q):
                SP.replay(q)
    return nc


def _prep_core_inputs(c, x, prm_common, wa, wb):
    b, s0 = c // 4, (c % 4) * TOK_PER_CORE
    xc = np.zeros((NBLK * 128, D), np.float32)
    lo = s0 - 128
    if lo < 0:
        xc[128:] = x[b, 0:TOK_PER_CORE]
    else:
        xc[:] = x[b, lo:s0 + TOK_PER_CORE]
    xl = np.ascontiguousarray(xc.reshape(NBLK, 128, NQ, 128).transpose(3, 0, 2, 1))
    prm = prm_common.copy()
    start = (lo < 0)
    prm[:, P_MASK:P_MASK + 128] = 0.0 if start else 1.0
    pm = np.zeros((128, 3, 4, 128), np.float32)
    tt = np.arange(128)
    for gi, w in enumerate(WINDOWS):
        dlt = tt[None, :] - tt[:, None]
        band = ((dlt >= 0) & (dlt < w)).astype(np.float32)
        pm[:, 0, gi, :] = band / w - np.eye(128, dtype=np.float32)
        dprev = tt[None, :] + 128 - tt[:, None]
        pm[:, 1, gi, :] = (dprev < w).astype(np.float32) / w
        if start:
            cntv = np.minimum(tt + 1, w).astype(np.float32)
            pm[:, 2, gi, :] = band / cntv[None, :] - np.eye(128, dtype=np.float32)
        else:
            pm[:, 2, gi, :] = pm[:, 0, gi, :]
    return {"x": xl, "wa": wa, "wb": wb, "prm": prm, "pm": np.ascontiguousarray(pm.reshape(128, 12 * 128))}


def _pq(v):
    return np.ascontiguousarray(np.asarray(v, np.float32).reshape(NQ, 128).T)


def kernel(x, norm_g, final_g, conv_w_in, conv_dw, conv_dw_b, conv_ln_g, conv_ln_b, conv_w_out,
           pool_w_in, pool_w_grp, pool_b_grp, pool_scale, pool_w_out):
    x = np.asarray(x, np.float32)
    prm = np.zeros((128, NPRM), np.float32)
    for l in range(4):
        prm[:, P_GAIN + l * 8:P_GAIN + l * 8 + 8] = _pq(norm_g[l])
    prm[:, P_GAIN + 32:P_GAIN + 40] = _pq(final_g)
    for j in range(2):
        prm[:, P_DWB + j * 8:P_DWB + j * 8 + 8] = _pq(conv_dw_b[j])
        prm[:, P_LNG + j * 8:P_LNG + j * 8 + 8] = _pq(conv_ln_g[j])
        prm[:, P_LNB + j * 8:P_LNB + j * 8 + 8] = _pq(conv_ln_b[j])
        prm[:, P_BG + j * 8:P_BG + j * 8 + 8] = _pq(pool_b_grp[j])
        prm[:, P_SC + j * 8:P_SC + j * 8 + 8] = _pq(pool_scale[j])
        dwp = np.zeros((32, D), np.float32)
        dwp[:31] = np.asarray(conv_dw[j], np.float32)[::-1]
        t5 = dwp.reshape(2, 16, NQ, 2, 64).transpose(0, 4, 2, 3, 1)
        prm[:, P_DW + j * 256:P_DW + j * 256 + 256] = t5.reshape(128, 256)
    prm[:, P_ID:P_ID + 64] = 0.5 * np.concatenate([np.eye(64, dtype=np.float32)] * 2, 0)
    wa = np.zeros((2, 128, WA_COLS), np.float32)
    wb = np.zeros((2, 128, WB_COLS), np.float32)
    for j in range(2):
        wa[j, :, :WA_OUT] = np.asarray(conv_w_in[j], np.float32).reshape(NQ, 128, 3072).transpose(1, 0, 2).reshape(128, -1)
        wa[j, :, WA_OUT:] = np.asarray(conv_w_out[j], np.float32).reshape(NQ, 128, 1024).transpose(1, 0, 2).reshape(128, -1)
        wb[j, :, :WB_GRP] = np.asarray(pool_w_in[j], np.float32).reshape(NQ, 128, 2048).transpose(1, 0, 2).reshape(128, -1)
        wb[j, :, WB_GRP:WB_OUT] = np.asarray(pool_w_grp[j], np.float32).reshape(4, 2, 128, 256).transpose(2, 0, 1, 3).reshape(128, -1)
        wb[j, :, WB_OUT:] = np.asarray(pool_w_out[j], np.float32).reshape(NQ, 128, 1024).transpose(1, 0, 2).reshape(128, -1)
    in_maps = [_prep_core_inputs(c, x, prm, wa, wb) for c in range(NCORES)]
    nc = build_program()
    res = run_bass_kernel_spmd(nc, in_maps, core_ids=list(range(NCORES)))
    out = np.empty((2, SEQ, D), np.float32)
    for c in range(NCORES):
        b, s0 = c // 4, (c % 4) * TOK_PER_CORE
        oc = np.asarray(res.results[c]["out"], np.float32)
        out[b, s0:s0 + TOK_PER_CORE] = oc.transpose(1, 3, 2, 0).reshape(TOK_PER_CORE, D)
    return out
```

```python
import numpy as np
from collections import defaultdict
import concourse.bass as bass
import concourse.mybir as mybir
from concourse.bass_utils import run_bass_kernel_spmd

F32 = mybir.dt.float32
BF16 = mybir.dt.bfloat16
ALU = mybir.AluOpType
AF = mybir.ActivationFunctionType

NCORES = 8
D = 1024
NQ = 8
SEQ = 16384
TOK_PER_CORE = 4096
NBLK = 33
H = 32
RMS_EPS = 1e-6
LN_EPS = 1e-5
WINDOWS = (2, 4, 8, 16)

TILES = [list(range(0, 7)), list(range(7, 14)), list(range(14, 21)),
         list(range(21, 27)), list(range(27, 33))]
XSLOTS = 7
DBG_LAYERS = 4
NTMP = 4
NWD = 3
NRB = 4
_LAST = {}


def _groups(tile_blocks):
    if tile_blocks[0] == 0:
        gs = [[0]]
        rest = tile_blocks[1:]
    else:
        gs = []
        rest = tile_blocks
    i = 0
    while i < len(rest):
        gs.append(rest[i:i + 2])
        i += 2
    return gs


P_GAIN = 0
P_DWB = 40
P_LNG = 56
P_LNB = 72
P_BG = 88
P_SC = 104
P_DW = 120
P_ID = 632
P_MASK = 696
NPRM = 824

WA_COLS = 8 * 3072 + 8 * 1024
WA_OUT = 8 * 3072
WB_COLS = 8 * 2048 + 4 * 2 * 256 + 8 * 1024
WB_GRP = 8 * 2048
WB_OUT = WB_GRP + 2048


class _Res:
    __slots__ = ("w", "r")

    def __init__(self):
        self.w = None
        self.r = {}


class _DSem:
    def __init__(self, sem):
        self.sem = sem
        self.cnt = 0


class _Q:
    def __init__(self, sem, res, inorder=False):
        self.sem = sem
        self.inorder = inorder
        self.cnt = 0
        self.waited = {}
        self.res = res
        self.prog = []

    def _wait(self, tok):
        if tok is None:
            return
        s, v = tok
        if self.inorder and s is self.sem:
            return
        k = id(s)
        if self.waited.get(k, 0) >= v:
            return
        self.prog.append(("w", s, v))
        self.waited[k] = v

    def deps(self, reads, writes):
        need = {}

        def add(tok):
            if tok is None:
                return
            k = id(tok[0])
            if k not in need or need[k][1] < tok[1]:
                need[k] = tok
        for r in reads:
            add(self.res[r].w)
        for w in writes:
            rr = self.res[w]
            if rr.r:
                for tok in rr.r.values():
                    add(tok)
            else:
                add(rr.w)
        for tok in need.values():
            self._wait(tok)

    def commit(self, tok, reads, writes):
        k = id(tok[0])
        for r in reads:
            rr = self.res[r].r
            if k not in rr or rr[k][1] < tok[1]:
                rr[k] = tok
        for w in writes:
            rr = self.res[w]
            rr.w = tok
            rr.r = {}

    def op(self, fn, reads=(), writes=()):
        self.deps(reads, writes)
        self.cnt += 1
        self.prog.append(("i", fn, self.sem, 1))
        self.commit((self.sem, self.cnt), reads, writes)

    def mm_group(self, mms, reads=(), writes=(), label=""):
        self.deps(reads, writes)
        for kw in mms[:-1]:
            self.prog.append(("m", kw, None, 0, label))
        self.cnt += 1
        self.prog.append(("m", mms[-1], self.sem, 1, label))
        self.commit((self.sem, self.cnt), reads, writes)

    def dma_group(self, dsem, fns, reads=(), writes=()):
        self.deps(reads, writes)
        for fn in fns:
            dsem.cnt += 16
            self.prog.append(("i", fn, dsem.sem, 16))
        self.commit((dsem.sem, dsem.cnt), reads, writes)

    def replay(self, q):
        for it in self.prog:
            if it[0] == "w":
                q.wait_ge(it[1], it[2])
            elif it[0] == "i":
                ins = it[1](q)
                if it[2] is not None:
                    ins.then_inc(it[2], it[3])
            else:
                ins = q.matmul(**it[1])
                if it[2] is not None:
                    ins.then_inc(it[2], it[3])


def build_program():
    nc = bass.Bass("TRN2", target_bir_lowering=False)
    x_d = nc.dram_tensor("x", [128, NBLK, NQ, 128], F32, kind="ExternalInput").ap()
    wa_d = nc.dram_tensor("wa", [2, 128, WA_COLS], F32, kind="ExternalInput").ap()
    wb_d = nc.dram_tensor("wb", [2, 128, WB_COLS], F32, kind="ExternalInput").ap()
    prm_d = nc.dram_tensor("prm", [128, NPRM], F32, kind="ExternalInput").ap()
    pm_d = nc.dram_tensor("pm", [128, 12 * 128], F32, kind="ExternalInput").ap()
    out_d = nc.dram_tensor("out", [128, NBLK - 1, NQ, 128], F32, kind="ExternalOutput").ap()
    wds_d = nc.dram_tensor("wds", [16, 128, 32 * 64], BF16, kind="Internal").ap()

    from contextlib import ExitStack
    with ExitStack() as es:
        def sb(name, shape, dt):
            return es.enter_context(nc.sbuf_tensor(name, shape, dt))

        def sem(name):
            return es.enter_context(nc.semaphore(name))

        x_sb = sb("x_sb", [128, XSLOTS, NQ, 128], F32)
        WA = sb("WA", [128, WA_COLS], BF16)
        WB = sb("WB", [128, WB_COLS], BF16)
        prm = sb("prm_sb", [128, NPRM], F32)
        PM = sb("PM", [128, 12, 128], BF16)
        ones = sb("ones", [128, 128], BF16)
        epsr = sb("epsr", [128, 1], F32)
        zero1 = sb("zero1", [128, 1], F32)
        bs = sb("bs", [128, 16], F32)
        hist = [sb(f"hist{j}", [128, NQ, H], BF16) for j in range(2)]
        prevT = [sb(f"prevT{j}", [128, D], BF16) for j in range(2)]
        hnb = sb("hnb", [128, NQ, 256], BF16)
        RW = H + 256 + 16
        RA = sb("RA", [128, NQ, 2, RW], BF16)
        szb = [sb(f"szb{i}", [128, NQ, 256], BF16) for i in range(2)]
        vb = sb("vb", [128, NQ, 256], BF16)
        gtb = sb("gtb", [128, NQ, 256], BF16)
        uT = sb("uT", [128, 2, D], BF16)
        Wd = [sb(f"Wd{i}", [128, 32, 64], BF16) for i in range(NWD)]
        Wd.append(uT[:].rearrange("p a (b c) -> p (a b) c", c=64))
        UT_ALIAS = ["uT0_0", "uT0_1", "uT1_0", "uT1_1"]
        st_a = sb("st_a", [128, 256], F32)
        st_b = sb("st_b", [128, 256], F32)
        st_c = sb("st_c", [128, 256], F32)
        tmp = [sb(f"tmp{i}", [128, 256], F32) for i in range(NTMP - 1)]
        tmp.append(prm[:, P_DW:P_DW + 256])
        ps = [es.enter_context(nc.psum_tensor(f"ps{i}", [128, 512], F32)) for i in range(8)]

        s_pe, s_act, s_dve, s_pool = sem("s_pe"), sem("s_act"), sem("s_dve"), sem("s_pool")
        d_const = _DSem(sem("d_const"))
        d_pm = _DSem(sem("d_pm"))
        d_wa = _DSem(sem("d_wa"))
        d_wb = _DSem(sem("d_wb"))
        d_x = [_DSem(sem(f"d_x{i}")) for i in range(XSLOTS)]
        d_out = [_DSem(sem(f"d_out{i}")) for i in range(4)]
        d_r = [_DSem(sem(f"d_r{i}")) for i in range(NQ)]
        d_prev = [_DSem(sem(f"d_prev{i}")) for i in range(2)]
        d_wdst = [_DSem(sem(f"d_wdst{i}")) for i in range(NWD)]
        d_wdld = [_DSem(sem(f"d_wdld{i}")) for i in range(NWD + 1)]

        res = defaultdict(_Res)
        PE = _Q(s_pe, res, inorder=True)
        ACT = _Q(s_act, res)
        DVE = _Q(s_dve, res)
        POOL = _Q(s_pool, res)
        SP = _Q(None, res)

        cnt = {"tmp": 0, "r": 0, "wd": 0}
        preload = {}

        def slot(b, h, N):
            return ps[b][:, h * 256:h * 256 + N]

        def pcol(c):
            return prm[:, c:c + 1]

        SP.dma_group(d_const, [lambda q: q.dma_start(out=prm[:], in_=prm_d[:])], writes=["prm"])
        POOL.dma_group(d_pm, [lambda q: q.dma_start(out=PM[:].rearrange("p a b -> p (a b)"), in_=pm_d[:])],
                       writes=["PM"])
        POOL.op(lambda q: q.memset(ones[:], 1.0 / D), writes=["ones"])
        POOL.op(lambda q: q.memset(epsr[:], RMS_EPS), writes=["eps"])
        POOL.op(lambda q: q.memset(zero1[:], 0.0), writes=["eps"])
        for j in range(2):
            POOL.op(lambda q, j=j: q.memset(hist[j][:], 0.0), writes=[f"hist{j}"])
            POOL.op(lambda q, j=j: q.memset(prevT[j][:], 0.0), writes=[f"prevT{j}"])
        POOL.op(lambda q: q.memset(RA[:], 0.0), writes=[f"uo{k}" for k in range(NQ)] + [f"uc{k}" for k in range(NQ)])
        DVE.op(lambda q: q.tensor_tensor(out=bs[:], in0=prm[:, P_BG:P_BG + 16], in1=prm[:, P_SC:P_SC + 16],
                                         op=ALU.mult), reads=["prm"], writes=["bs"])

        PIECE = 2048
        WA_RES = [f"WA{i}" for i in range(WA_COLS // PIECE)]
        WB_RES = [f"WB{i}" for i in range(WB_COLS // PIECE)]
        pending = []

        def queue_weights(kind, j):
            if kind == "A":
                for i in range(WA_COLS // PIECE):
                    pending.append(("A", j, i))
            else:
                for i in range(WB_COLS // PIECE):
                    pending.append(("B", j, i))

        def pump(n=1):
            for _ in range(n):
                if not pending:
                    return
                kind, j, i = pending.pop(0)
                a, b = i * PIECE, (i + 1) * PIECE
                if kind == "A":
                    POOL.dma_group(d_wa, [lambda q, a=a, b=b, j=j: q.dma_start(out=WA[:, a:b], in_=wa_d[j, :, a:b])],
                                   writes=[f"WA{i}"])
                else:
                    POOL.dma_group(d_wb, [lambda q, a=a, b=b, j=j: q.dma_start(out=WB[:, a:b], in_=wb_d[j, :, a:b])],
                                   writes=[f"WB{i}"])

        def xview(k0, nb):
            return x_sb[:, k0:k0 + nb, :, :]

        def rms_stage(L, k0, nb, xres):
            N = 128 * nb
            ACT.op(lambda q: q.activation(out=hnb[:, :, 0:N].rearrange("p q (b t) -> p b q t", b=nb),
                                          in_=xview(k0, nb), func=AF.Square),
                   reads=xres, writes=["hn"])
            PE.mm_group([dict(out=slot(6, 0, N), lhsT=ones[:], rhs=hnb[:, q, 0:N],
                              start=(q == 0), stop=(q == NQ - 1)) for q in range(NQ)],
                        reads=["hn", "ones"], writes=["ps6"], label="rms_ss")
            ACT.op(lambda q: q.activation(out=st_a[:, 0:N], in_=slot(6, 0, N), func=AF.Sqrt,
                                          bias=epsr[:, 0:1], scale=1.0),
                   reads=["ps6", "eps"], writes=["st_a"])

        def hn_stage(L, k0, nb, xres):
            N = 128 * nb
            DVE.op(lambda q: q.reciprocal(out=st_a[:, 0:N], in_=st_a[:, 0:N]), reads=["st_a"], writes=["st_a"])
            for qq in range(NQ):
                DVE.op(lambda q, qq=qq: q.scalar_tensor_tensor(
                    out=hnb[:, qq, 0:N].rearrange("p (b t) -> p b t", b=nb),
                    in0=x_sb[:, k0:k0 + nb, qq, :], scalar=pcol(P_GAIN + L * 8 + qq),
                    in1=st_a[:, 0:N].rearrange("p (b t) -> p b t", b=nb),
                    op0=ALU.mult, op1=ALU.mult),
                    reads=xres + ["st_a", "prm"], writes=["hn"])

        def outproj_stage(Wt, wres, base, k0, nb, xres, mask_halo, mid=None):
            N = 128 * nb
            for dq in range(NQ):
                if dq == 4 and mid is not None:
                    mid()
                b, h = (7 if dq % 2 == 0 else 5), 0
                PE.mm_group([dict(out=slot(b, h, N), lhsT=Wt[:, base + eq * 1024 + dq * 128: base + eq * 1024 + dq * 128 + 128],
                                  rhs=gtb[:, eq, 0:N], start=(eq == 0), stop=(eq == NQ - 1)) for eq in range(NQ)],
                            reads=wres + [f"gt{e}" for e in range(NQ)], writes=[f"ps{b}"], label=f"outproj{dq}")
                DVE.op(lambda q, dq=dq, b=b, h=h: q.tensor_tensor(
                    out=x_sb[:, k0:k0 + nb, dq, :], in0=x_sb[:, k0:k0 + nb, dq, :],
                    in1=slot(b, h, N).rearrange("p (b t) -> p b t", b=nb), op=ALU.add),
                    reads=[f"ps{b}"] + xres, writes=xres)
            if mask_halo:
                DVE.op(lambda q: q.tensor_tensor(
                    out=x_sb[:, 0, :, :], in0=x_sb[:, 0, :, :],
                    in1=prm[:, P_MASK:P_MASK + 128].unsqueeze(1).to_broadcast([128, NQ, 128]), op=ALU.mult),
                    reads=["x0", "prm"], writes=["x0"])

        class Item:
            pass

        def xres_of(it):
            return [f"x{s}" for s in range(it.k0, it.k0 + it.nb)]

        def phase1a(it):
            rms_stage(it.L, it.k0, it.nb, xres_of(it))

        def phase1b(it):
            xres = xres_of(it)
            k0, nb, N = it.k0, it.nb, 128 * it.nb
            if it.kind != "final":
                hn_stage(it.L, k0, nb, xres)
                return
            DVE.op(lambda q: q.reciprocal(out=st_a[:, 0:N], in_=st_a[:, 0:N]), reads=["st_a"], writes=["st_a"])
            for qq in range(NQ):
                DVE.op(lambda q, qq=qq: q.scalar_tensor_tensor(
                    out=x_sb[:, k0:k0 + nb, qq, :], in0=x_sb[:, k0:k0 + nb, qq, :], scalar=pcol(P_GAIN + 32 + qq),
                    in1=st_a[:, 0:N].rearrange("p (b t) -> p b t", b=nb), op0=ALU.mult, op1=ALU.mult),
                    reads=xres + ["st_a", "prm"], writes=xres)
            SP.dma_group(d_out[it.gidx], [lambda q: q.dma_start(out=out_d[:, it.blk0 - 1:it.blk0 - 1 + nb, :, :],
                                                                in_=x_sb[:, k0:k0 + nb, :, :])],
                         reads=xres, writes=[])
            if it.t + 1 < len(TILES):
                slots = list(range(k0, k0 + nb))
                if it.t == 0 and it.gidx == 1:
                    slots = [0] + slots
                for s_ in slots:
                    if s_ < len(TILES[it.t + 1]):
                        load_x_slot(it.t + 1, s_)

        def phase2(it, fill):
            nunits = 8 if it.kind == "conv" else (2 * it.nb + 8)
            nfill = len(fill)
            state = {"u": 0}

            def pop():
                pump(2)
                state["u"] += 1
                target = nfill if state["u"] >= nunits - 2 else (nfill * state["u"] + nunits - 3) // max(1, nunits - 2)
                while fill and nfill - len(fill) < target:
                    fill.pop(0)()
            if it.kind == "final":
                while fill:
                    pop()
                return
            j = it.L // 2
            k0, nb, N = it.k0, it.nb, 128 * it.nb
            sz = szb[it.par]
            szr = f"sz{it.par}_"
            if it.kind == "conv":
                ures = [f"uo{q}" for q in range(NQ)]
                POOL.op(lambda q: q.tensor_copy(out=RA[0:64, :, 0, 0:H], in_=hist[j][0:64, :, :]),
                        reads=[f"hist{j}"], writes=ures)
                POOL.op(lambda q: q.tensor_copy(out=RA[64:128, :, 1, 16:16 + H], in_=hist[j][64:128, :, :]),
                        reads=[f"hist{j}"], writes=ures)
                for qq in range(NQ):
                    abk = (0, 1, 4, 7)[qq % 4]
                    zbk = (2, 3, 5, 6)[qq % 4]
                    for part, bank, half in ((0, abk, 0), (1, abk, 1), (2, zbk, 0)):
                        c0 = part * 1024 + qq * 128
                        PE.mm_group([dict(out=slot(bank, half, N), lhsT=WA[:, kq * 3072 + c0: kq * 3072 + c0 + 128],
                                          rhs=hnb[:, kq, 0:N], start=(kq == 0), stop=(kq == NQ - 1))
                                     for kq in range(NQ)], reads=["hn"] + WA_RES, writes=[f"ps{bank}"], label=f"c.inproj{qq}.{part}")
                    ti = cnt["tmp"] % NTMP
                    cnt["tmp"] += 1
                    ACT.op(lambda q, ti=ti, abk=abk: q.activation(out=tmp[ti][:, 0:N], in_=slot(abk, 1, N),
                                                                 func=AF.Tanh, scale=0.5),
                           reads=[f"ps{abk}"], writes=[f"tmp{ti}"])
                    DVE.op(lambda q, ti=ti, abk=abk, qq=qq: q.scalar_tensor_tensor(
                        out=RA[0:64, qq, 0, H:H + N], in0=tmp[ti][0:64, 0:N], scalar=1.0, in1=slot(abk, 0, N)[0:64, :],
                        op0=ALU.add, op1=ALU.mult),
                        reads=[f"tmp{ti}", f"ps{abk}"], writes=[f"uo{qq}"])
                    DVE.op(lambda q, ti=ti, abk=abk, qq=qq: q.scalar_tensor_tensor(
                        out=RA[64:128, qq, 1, H + 16:H + 16 + N], in0=tmp[ti][64:128, 0:N], scalar=1.0,
                        in1=slot(abk, 0, N)[64:128, :], op0=ALU.add, op1=ALU.mult),
                        reads=[f"tmp{ti}", f"ps{abk}"], writes=[f"uo{qq}"])
                    SP.dma_group(d_r[qq], [
                        lambda q, qq=qq: q.dma_start(out=RA[0:64, qq, 1, 0:H + N], in_=RA[64:128, qq, 1, 16:16 + H + N]),
                        lambda q, qq=qq: q.dma_start(out=RA[64:128, qq, 0, 16:16 + H + N], in_=RA[0:64, qq, 0, 0:H + N])],
                        reads=[f"uo{qq}"], writes=[f"uc{qq}"])
                    ACT.op(lambda q, zbk=zbk, qq=qq: q.activation(out=sz[:, qq, 0:N], in_=slot(zbk, 0, N), func=AF.Silu),
                           reads=[f"ps{zbk}"], writes=[szr + str(qq)])
                    pop()
                POOL.op(lambda q: q.tensor_copy(out=hist[j][0:64, :, :], in_=RA[0:64, :, 0, N:N + H]),
                        reads=ures, writes=[f"hist{j}"])
                POOL.op(lambda q: q.tensor_copy(out=hist[j][64:128, :, :], in_=RA[64:128, :, 1, N + 16:N + 16 + H]),
                        reads=ures, writes=[f"hist{j}"])
            else:
                for bb in range(nb):
                    for hh in range(2):
                        bank = (0, 1, 4, 7)[(bb * 2 + hh) % 4]
                        PE.mm_group([dict(out=ps[bank][:, :], lhsT=hnb[:, kq, bb * 128:(bb + 1) * 128],
                                          rhs=WB[:, kq * 2048 + hh * 512: kq * 2048 + hh * 512 + 512],
                                          start=(kq == 0), stop=(kq == NQ - 1)) for kq in range(NQ)],
                                    reads=["hn"] + WB_RES, writes=[f"ps{bank}"], label=f"p.uT{bb}{hh}")
                        ACT.op(lambda q, bb=bb, hh=hh, bank=bank: q.activation(
                            out=uT[:, bb, hh * 512:(hh + 1) * 512], in_=ps[bank][:, :], func=AF.Copy),
                            reads=[f"ps{bank}"], writes=[f"uT{bb}_{hh}"])
                        pop()
                for qq in range(NQ):
                    zbk = (2, 3, 5, 6)[qq % 4]
                    c0 = 1024 + qq * 128
                    PE.mm_group([dict(out=slot(zbk, 0, N), lhsT=WB[:, kq * 2048 + c0: kq * 2048 + c0 + 128],
                                      rhs=hnb[:, kq, 0:N], start=(kq == 0), stop=(kq == NQ - 1)) for kq in range(NQ)],
                                reads=["hn"] + WB_RES, writes=[f"ps{zbk}"], label=f"p.z{qq}")
                    ACT.op(lambda q, zbk=zbk, qq=qq: q.activation(out=sz[:, qq, 0:N], in_=slot(zbk, 0, N), func=AF.Silu),
                           reads=[f"ps{zbk}"], writes=[szr + str(qq)])
                    pop()
            while fill:
                pop()

        def phase3(it, mid):
            if it.kind == "final":
                mid()
                return
            if it.kind == "conv":
                mid()
            j = it.L // 2
            k0, nb, N = it.k0, it.nb, 128 * it.nb
            sz = szb[it.par]
            szr = f"sz{it.par}_"
            if it.kind == "conv":
                wsel = preload.pop(id(it), {})

                def wd_load(qq, jj=j, sel=None):
                    sel = wsel if sel is None else sel
                    wp = cnt["wd"] % (NWD + 1)
                    cnt["wd"] += 1
                    wdres = [f"Wd{wp}"] + (UT_ALIAS if wp == NWD else [])
                    sel[qq] = (wp, wdres)
                    widx = jj * 8 + qq
                    SP.dma_group(d_wdld[wp], [lambda q, wp=wp, widx=widx, hf=hf: q.dma_start(
                        out=Wd[wp][:, 16 * hf:16 * hf + 16, :].rearrange("p a b -> p (a b)"),
                        in_=wds_d[widx][:, 1024 * hf:1024 * hf + 1024]) for hf in range(2)],
                        reads=[f"wds{widx}"], writes=wdres)

                for qq in range(4):
                    if qq not in wsel:
                        wd_load(qq)
                for qq in range(NQ):
                    par = qq % 2
                    wp, wdres = wsel[qq]
                    mms = []
                    for kk in range(16):
                        for m in range(2):
                            mms.append(dict(out=slot(4 + par, 0, N)[64 * m:64 * m + 64, :], lhsT=Wd[wp][:, m * 16 + kk, :],
                                            rhs=RA[:, qq, m, H - kk:H - kk + N], start=(kk == 0), stop=(kk == 15),
                                            skip_group_check=True, tile_position=(0, 64 * m)))
                    PE.mm_group(mms, reads=[f"uo{qq}", f"uc{qq}"] + wdres, writes=[f"ps{4 + par}"], label=f"c.conv{qq}")
                    if qq + 4 < NQ:
                        wd_load(qq + 4)
                    else:
                        nci = it.next_conv
                        if nci is not None and (nci is it.next_item or qq < 7):
                            wd_load(qq - 4, jj=nci.L // 2, sel=preload.setdefault(id(nci), {}))
                    ACT.op(lambda q, par=par, qq=qq: q.activation(out=vb[:, qq, 0:N], in_=slot(4 + par, 0, N), func=AF.Identity,
                                                                 bias=pcol(P_DWB + j * 8 + qq), scale=1.0),
                           reads=[f"ps{4 + par}", "prm"], writes=[f"v{qq}"])
                    ACT.op(lambda q, par=par, qq=qq: q.activation(out=gtb[:, qq, 0:N], in_=slot(4 + par, 0, N), func=AF.Square,
                                                                 bias=pcol(P_DWB + j * 8 + qq), scale=1.0),
                           reads=[f"ps{4 + par}", "prm"], writes=[f"gt{qq}"])
                vres = [f"v{q}" for q in range(NQ)]
                gres = [f"gt{q}" for q in range(NQ)]
                PE.mm_group([dict(out=slot(6, 0, N), lhsT=ones[:], rhs=vb[:, q, 0:N], start=(q == 0), stop=(q == NQ - 1))
                             for q in range(NQ)], reads=vres + ["ones"], writes=["ps6"], label="c.ln_s1")
                PE.mm_group([dict(out=slot(6, 1, N), lhsT=ones[:], rhs=gtb[:, q, 0:N], start=(q == 0), stop=(q == NQ - 1))
                             for q in range(NQ)], reads=gres + ["ones"], writes=["ps6"], label="c.ln_s2")
            else:
                for qq in range(NQ):
                    par = qq % 2
                    dbk = (4, 5, 7, 6)[qq % 4]
                    gi = qq // 2
                    hh = qq // 4
                    mms = []
                    rd = ["PM"]
                    for bb in range(nb):
                        kind = 2 if (it.first_blk and bb == 0) else 0
                        o = slot(dbk, 0, N)[:, bb * 128:(bb + 1) * 128]
                        mms.append(dict(out=o, lhsT=uT[:, bb, qq * 128:(qq + 1) * 128], rhs=PM[:, kind * 4 + gi, :],
                                        start=True, stop=False))
                        if bb == 0:
                            pl = prevT[j][:, qq * 128:(qq + 1) * 128]
                            rd.append(f"prevT{j}")
                        else:
                            pl = uT[:, bb - 1, qq * 128:(qq + 1) * 128]
                        mms.append(dict(out=o, lhsT=pl, rhs=PM[:, 4 + gi, :], start=False, stop=True))
                        rd.append(f"uT{bb}_{hh}")
                    PE.mm_group(mms, reads=rd, writes=[f"ps{dbk}"], label=f"p.pool{qq}")
                    ACT.op(lambda q, dbk=dbk, qq=qq: q.activation(out=vb[:, qq, 0:N], in_=slot(dbk, 0, N), func=AF.Copy),
                           reads=[f"ps{dbk}"], writes=[f"v{qq}"])
                mid()
                SP.dma_group(d_prev[j], [lambda q: q.dma_start(out=prevT[j][:, :], in_=uT[:, nb - 1, :])],
                             reads=[f"uT{nb - 1}_0", f"uT{nb - 1}_1"], writes=[f"prevT{j}"])
                for cq in range(NQ):
                    par = cq % 2
                    gi = cq // 2
                    mmi = cq % 2
                    base = WB_GRP + gi * 512
                    yb = cq % 4
                    PE.mm_group([dict(out=slot(yb, 0, N), lhsT=WB[:, base + kk * 256 + mmi * 128: base + kk * 256 + mmi * 128 + 128],
                                      rhs=vb[:, 2 * gi + kk, 0:N], start=(kk == 0), stop=(kk == 1)) for kk in range(2)],
                                reads=WB_RES + [f"v{2 * gi}", f"v{2 * gi + 1}"], writes=[f"ps{yb}"], label=f"p.grp{cq}")
                    ACT.op(lambda q, yb=yb, cq=cq: q.activation(
                        out=gtb[:, cq, 0:N], in_=slot(yb, 0, N), func=AF.Identity,
                        bias=bs[:, j * 8 + cq:j * 8 + cq + 1], scale=pcol(P_SC + j * 8 + cq)),
                        reads=[f"ps{yb}", "bs", "prm"], writes=[f"gt{cq}"])

        def phase4_steps(it):
            if it.kind == "final":
                return []
            j = it.L // 2
            N = 128 * it.nb
            sz = szb[it.par]
            szr = f"sz{it.par}_"
            if it.kind == "pool":
                def mkp(h):
                    def step():
                        DVE.op(lambda q: q.tensor_tensor(out=gtb[:, 4 * h:4 * h + 4, 0:N], in0=gtb[:, 4 * h:4 * h + 4, 0:N],
                                                         in1=sz[:, 4 * h:4 * h + 4, 0:N], op=ALU.mult),
                               reads=[f"gt{c}" for c in range(4 * h, 4 * h + 4)] + [szr + str(c) for c in range(4 * h, 4 * h + 4)],
                               writes=[f"gt{c}" for c in range(4 * h, 4 * h + 4)])
                    return step
                return [mkp(0), mkp(1)]

            def chain():
                ACT.op(lambda q: q.activation(out=st_b[:, 0:N], in_=slot(6, 0, N), func=AF.Square),
                       reads=["ps6"], writes=["st_b"])
                DVE.op(lambda q: q.scalar_tensor_tensor(out=st_c[:, 0:N], in0=slot(6, 1, N), scalar=LN_EPS,
                                                        in1=st_b[:, 0:N], op0=ALU.add, op1=ALU.subtract),
                       reads=["ps6", "st_b"], writes=["st_c"])
                ACT.op(lambda q: q.activation(out=st_c[:, 0:N], in_=st_c[:, 0:N], func=AF.Sqrt,
                                              bias=zero1[:, 0:1], scale=1.0),
                       reads=["st_c", "eps"], writes=["st_c"])
                DVE.op(lambda q: q.reciprocal(out=st_c[:, 0:N], in_=st_c[:, 0:N]), reads=["st_c"], writes=["st_c"])
                DVE.op(lambda q: q.scalar_tensor_tensor(out=st_b[:, 0:N], in0=slot(6, 0, N), scalar=-1.0,
                                                        in1=st_c[:, 0:N], op0=ALU.mult, op1=ALU.mult),
                       reads=["ps6", "st_c"], writes=["st_b"])

            tmps = {}

            def A(qq):
                ti = cnt["tmp"] % NTMP
                cnt["tmp"] += 1
                tmps[qq] = ti
                DVE.op(lambda q: q.tensor_tensor(out=tmp[ti][:, 0:N], in0=vb[:, qq, 0:N],
                                                 in1=st_c[:, 0:N], op=ALU.mult),
                       reads=[f"v{qq}", "st_c"], writes=[f"tmp{ti}"])
                DVE.op(lambda q: q.tensor_tensor(out=tmp[ti][:, 0:N], in0=tmp[ti][:, 0:N],
                                                 in1=st_b[:, 0:N], op=ALU.add),
                       reads=[f"tmp{ti}", "st_b"], writes=[f"tmp{ti}"])

            def B(qq):
                ti = tmps[qq]
                ACT.op(lambda q: q.activation(out=gtb[:, qq, 0:N], in_=tmp[ti][:, 0:N], func=AF.Silu,
                                              bias=pcol(P_LNB + j * 8 + qq), scale=pcol(P_LNG + j * 8 + qq)),
                       reads=[f"tmp{ti}", "prm"], writes=[f"gt{qq}"])

            def C(qq):
                DVE.op(lambda q: q.tensor_tensor(out=gtb[:, qq, 0:N], in0=gtb[:, qq, 0:N],
                                                 in1=sz[:, qq, 0:N], op=ALU.mult),
                       reads=[f"gt{qq}", szr + str(qq)], writes=[f"gt{qq}"])

            def mk(e):
                def step():
                    if 0 <= e - 2 < NQ:
                        C(e - 2)
                    if 0 <= e - 1 < NQ:
                        B(e - 1)
                    if 0 <= e < NQ:
                        A(e)
                return step
            return [chain] + [mk(e) for e in range(NQ + 2)]

        def phase5(it, mid=None):
            if it.kind == "final":
                if mid is not None:
                    mid()
                return
            if it.kind == "conv":
                outproj_stage(WA, WA_RES, WA_OUT, it.k0, it.nb, xres_of(it), it.mask_halo, mid)
            else:
                outproj_stage(WB, WB_RES, WB_OUT, it.k0, it.nb, xres_of(it), it.mask_halo, mid)

        items = []
        nlay = DBG_LAYERS
        for t, blocks in enumerate(TILES):
            groups = _groups(blocks)
            for L in list(range(nlay)) + [4]:
                for gi, g in enumerate(groups):
                    it = Item()
                    it.kind = "final" if L == 4 else ("conv" if L % 2 == 0 else "pool")
                    it.L, it.t, it.gidx = L, t, gi
                    it.k0, it.nb, it.blk0 = g[0] - blocks[0], len(g), g[0]
                    it.mask_halo = (g[0] == 0)
                    it.first_blk = (g[0] == 1)
                    it.par = len(items) % 2
                    it.last_of_step = (gi == len(groups) - 1)
                    it.first_of_tile = (L == 0 and gi == 0)
                    if it.kind == "final" and g[0] == 0:
                        continue
                    items.append(it)

        last_conv = None
        nxt_item = None
        for it_ in reversed(items):
            it_.next_conv = last_conv
            it_.next_item = nxt_item
            nxt_item = it_
            if it_.kind == "conv":
                last_conv = it_

        def load_x_slot(t, s_):
            blk = TILES[t][s_]
            SP.dma_group(d_x[s_], [lambda q: q.dma_start(out=x_sb[:, s_, :, :], in_=x_d[:, blk, :, :])],
                         writes=[f"x{s_}"])

        def load_x(t):
            for s_ in range(len(TILES[t])):
                load_x_slot(t, s_)

        step_list = [(t, L) for t in range(len(TILES)) for L in range(nlay)]

        def after_phase5(it):
            if it.kind == "final" or not it.last_of_step:
                return
            si = step_list.index((it.t, it.L))
            pump(1000)
            if si + 2 < len(step_list):
                t2, L2 = step_list[si + 2]
                queue_weights("A" if L2 % 2 == 0 else "B", L2 // 2)

        queue_weights("A", 0)
        pump(1000)
        for idx in range(16):
            wp = idx % NWD
            dwc = P_DW + idx * 32
            POOL.op(lambda q, wp=wp, dwc=dwc: q.tensor_tensor(
                out=Wd[wp][:], in0=prm[:, P_ID:P_ID + 64].unsqueeze(1).to_broadcast([128, 32, 64]),
                in1=prm[:, dwc:dwc + 32].unsqueeze(2).to_broadcast([128, 32, 64]), op=ALU.mult),
                reads=["prm"], writes=[f"Wd{wp}"])
            SP.dma_group(d_wdst[wp], [lambda q, wp=wp, idx=idx: q.dma_start(
                out=wds_d[idx], in_=Wd[wp][:].rearrange("p a b -> p (a b)"))],
                reads=[f"Wd{wp}"], writes=[f"wds{idx}"])

        res[f"tmp{NTMP - 1}"].r[id(s_pool)] = (s_pool, POOL.cnt)
        if nlay > 1:
            queue_weights("B", 0)
        pump(1000)
        load_x(0)
        phase1a(items[0])
        phase1b(items[0])
        phase2(items[0], [])
        if len(items) > 1:
            phase1a(items[1])
            phase1b(items[1])
        for i, it in enumerate(items):
            nxt = items[i + 1] if i + 1 < len(items) else None
            nxt2 = items[i + 2] if i + 2 < len(items) else None
            phase3(it, lambda: None)
            fill = phase4_steps(it)
            if nxt is not None:
                phase2(nxt, fill)
            while fill:
                fill.pop(0)()
            phase5(it, (lambda: phase1a(nxt2)) if nxt2 is not None else None)
            if nxt2 is not None:
                phase1b(nxt2)
            after_phase5(it)
        for ds in d_out + d_prev + d_r + d_wdld + d_wdst + d_x + [d_wa, d_wb, d_const, d_pm]:
            if ds.cnt:
                SP.prog.append(("w", ds.sem, ds.cnt))

        _LAST["PE"] = PE.prog
        with nc.Block() as block:
            @block.tensor
            def _(q):
                PE.replay(q)

            @block.scalar
            def _(q):
                ACT.replay(q)

            @block.vector
            def _(q):
                DVE.replay(q)

            @block.gpsimd
            def _(q):
                POOL.replay(q)

            @block.sync
            def _(q):
                SP.replay(q)
    return nc


def _prep_core_inputs(c, x, prm_common, wa, wb):
    b, s0 = c // 4, (c % 4) * TOK_PER_CORE
    xc = np.zeros((NBLK * 128, D), np.float32)
    lo = s0 - 128
    if lo < 0:
        xc[128:] = x[b, 0:TOK_PER_CORE]
    else:
        xc[:] = x[b, lo:s0 + TOK_PER_CORE]
    xl = np.ascontiguousarray(xc.reshape(NBLK, 128, NQ, 128).transpose(3, 0, 2, 1))
    prm = prm_common.copy()
    start = (lo < 0)
    prm[:, P_MASK:P_MASK + 128] = 0.0 if start else 1.0
    pm = np.zeros((128, 3, 4, 128), np.float32)
    tt = np.arange(128)
    for gi, w in enumerate(WINDOWS):
        dlt = tt[None, :] - tt[:, None]
        band = ((dlt >= 0) & (dlt < w)).astype(np.float32)
        pm[:, 0, gi, :] = band / w - np.eye(128, dtype=np.float32)
        dprev = tt[None, :] + 128 - tt[:, None]
        pm[:, 1, gi, :] = (dprev < w).astype(np.float32) / w
        if start:
            cntv = np.minimum(tt + 1, w).astype(np.float32)
            pm[:, 2, gi, :] = band / cntv[None, :] - np.eye(128, dtype=np.float32)
        else:
            pm[:, 2, gi, :] = pm[:, 0, gi, :]
    return {"x": xl, "wa": wa, "wb": wb, "prm": prm, "pm": np.ascontiguousarray(pm.reshape(128, 12 * 128))}


def _pq(v):
    return np.ascontiguousarray(np.asarray(v, np.float32).reshape(NQ, 128).T)


def kernel(x, norm_g, final_g, conv_w_in, conv_dw, conv_dw_b, conv_ln_g, conv_ln_b, conv_w_out,
           pool_w_in, pool_w_grp, pool_b_grp, pool_scale, pool_w_out):
    x = np.asarray(x, np.float32)
    prm = np.zeros((128, NPRM), np.float32)
    for l in range(4):
        prm[:, P_GAIN + l * 8:P_GAIN + l * 8 + 8] = _pq(norm_g[l])
    prm[:, P_GAIN + 32:P_GAIN + 40] = _pq(final_g)
    for j in range(2):
        prm[:, P_DWB + j * 8:P_DWB + j * 8 + 8] = _pq(conv_dw_b[j])
        prm[:, P_LNG + j * 8:P_LNG + j * 8 + 8] = _pq(conv_ln_g[j])
        prm[:, P_LNB + j * 8:P_LNB + j * 8 + 8] = _pq(conv_ln_b[j])
        prm[:, P_BG + j * 8:P_BG + j * 8 + 8] = _pq(pool_b_grp[j])
        prm[:, P_SC + j * 8:P_SC + j * 8 + 8] = _pq(pool_scale[j])
        dwp = np.zeros((32, D), np.float32)
        dwp[:31] = np.asarray(conv_dw[j], np.float32)[::-1]
        t5 = dwp.reshape(2, 16, NQ, 2, 64).transpose(0, 4, 2, 3, 1)
        prm[:, P_DW + j * 256:P_DW + j * 256 + 256] = t5.reshape(128, 256)
    prm[:, P_ID:P_ID + 64] = 0.5 * np.concatenate([np.eye(64, dtype=np.float32)] * 2, 0)
    wa = np.zeros((2, 128, WA_COLS), np.float32)
    wb = np.zeros((2, 128, WB_COLS), np.float32)
    for j in range(2):
        wa[j, :, :WA_OUT] = np.asarray(conv_w_in[j], np.float32).reshape(NQ, 128, 3072).transpose(1, 0, 2).reshape(128, -1)
        wa[j, :, WA_OUT:] = np.asarray(conv_w_out[j], np.float32).reshape(NQ, 128, 1024).transpose(1, 0, 2).reshape(128, -1)
        wb[j, :, :WB_GRP] = np.asarray(pool_w_in[j], np.float32).reshape(NQ, 128, 2048).transpose(1, 0, 2).reshape(128, -1)
        wb[j, :, WB_GRP:WB_OUT] = np.asarray(pool_w_grp[j], np.float32).reshape(4, 2, 128, 256).transpose(2, 0, 1, 3).reshape(128, -1)
        wb[j, :, WB_OUT:] = np.asarray(pool_w_out[j], np.float32).reshape(NQ, 128, 1024).transpose(1, 0, 2).reshape(128, -1)
    in_maps = [_prep_core_inputs(c, x, prm, wa, wb) for c in range(NCORES)]
    nc = build_program()
    res = run_bass_kernel_spmd(nc, in_maps, core_ids=list(range(NCORES)))
    out = np.empty((2, SEQ, D), np.float32)
    for c in range(NCORES):
        b, s0 = c // 4, (c % 4) * TOK_PER_CORE
        oc = np.asarray(res.results[c]["out"], np.float32)
        out[b, s0:s0 + TOK_PER_CORE] = oc.transpose(1, 3, 2, 0).reshape(TOK_PER_CORE, D)
    return out
```

```python
import numpy as np
from collections import defaultdict
import concourse.bass as bass
import concourse.mybir as mybir
from concourse.bass_utils import run_bass_kernel_spmd

F32 = mybir.dt.float32
BF16 = mybir.dt.bfloat16
ALU = mybir.AluOpType
AF = mybir.ActivationFunctionType

NCORES = 8
D = 1024
NQ = 8
SEQ = 16384
TOK_PER_CORE = 4096
NBLK = 33
H = 32
RMS_EPS = 1e-6
LN_EPS = 1e-5
WINDOWS = (2, 4, 8, 16)

TILES = [list(range(0, 7)), list(range(7, 14)), list(range(14, 21)),
         list(range(21, 27)), list(range(27, 33))]
XSLOTS = 7
DBG_LAYERS = 4
NTMP = 4
NWD = 3
NRB = 4
_LAST = {}


def _groups(tile_blocks):
    if tile_blocks[0] == 0:
        gs = [[0]]
        rest = tile_blocks[1:]
    else:
        gs = []
        rest = tile_blocks
    i = 0
    while i < len(rest):
        gs.append(rest[i:i + 2])
        i += 2
    return gs


P_GAIN = 0
P_DWB = 40
P_LNG = 56
P_LNB = 72
P_BG = 88
P_SC = 104
P_DW = 120
P_ID = 632
P_MASK = 696
NPRM = 824

WA_COLS = 8 * 3072 + 8 * 1024
WA_OUT = 8 * 3072
WB_COLS = 8 * 2048 + 4 * 2 * 256 + 8 * 1024
WB_GRP = 8 * 2048
WB_OUT = WB_GRP + 2048


class _Res:
    __slots__ = ("w", "r")

    def __init__(self):
        self.w = None
        self.r = {}


class _DSem:
    def __init__(self, sem):
        self.sem = sem
        self.cnt = 0


class _Q:
    def __init__(self, sem, res, inorder=False):
        self.sem = sem
        self.inorder = inorder
        self.cnt = 0
        self.waited = {}
        self.res = res
        self.prog = []

    def _wait(self, tok):
        if tok is None:
            return
        s, v = tok
        if self.inorder and s is self.sem:
            return
        k = id(s)
        if self.waited.get(k, 0) >= v:
            return
        self.prog.append(("w", s, v))
        self.waited[k] = v

    def deps(self, reads, writes):
        need = {}

        def add(tok):
            if tok is None:
                return
            k = id(tok[0])
            if k not in need or need[k][1] < tok[1]:
                need[k] = tok
        for r in reads:
            add(self.res[r].w)
        for w in writes:
            rr = self.res[w]
            if rr.r:
                for tok in rr.r.values():
                    add(tok)
            else:
                add(rr.w)
        for tok in need.values():
            self._wait(tok)

    def commit(self, tok, reads, writes):
        k = id(tok[0])
        for r in reads:
            rr = self.res[r].r
            if k not in rr or rr[k][1] < tok[1]:
                rr[k] = tok
        for w in writes:
            rr = self.res[w]
            rr.w = tok
            rr.r = {}

    def op(self, fn, reads=(), writes=()):
        self.deps(reads, writes)
        self.cnt += 1
        self.prog.append(("i", fn, self.sem, 1))
        self.commit((self.sem, self.cnt), reads, writes)

    def mm_group(self, mms, reads=(), writes=(), label=""):
        self.deps(reads, writes)
        for kw in mms[:-1]:
            self.prog.append(("m", kw, None, 0, label))
        self.cnt += 1
        self.prog.append(("m", mms[-1], self.sem, 1, label))
        self.commit((self.sem, self.cnt), reads, writes)

    def dma_group(self, dsem, fns, reads=(), writes=()):
        self.deps(reads, writes)
        for fn in fns:
            dsem.cnt += 16
            self.prog.append(("i", fn, dsem.sem, 16))
        self.commit((dsem.sem, dsem.cnt), reads, writes)

    def replay(self, q):
        for it in self.prog:
            if it[0] == "w":
                q.wait_ge(it[1], it[2])
            elif it[0] == "i":
                ins = it[1](q)
                if it[2] is not None:
                    ins.then_inc(it[2], it[3])
            else:
                ins = q.matmul(**it[1])
                if it[2] is not None:
                    ins.then_inc(it[2], it[3])


def build_program():
    nc = bass.Bass("TRN2", target_bir_lowering=False)
    x_d = nc.dram_tensor("x", [128, NBLK, NQ, 128], F32, kind="ExternalInput").ap()
    wa_d = nc.dram_tensor("wa", [2, 128, WA_COLS], F32, kind="ExternalInput").ap()
    wb_d = nc.dram_tensor("wb", [2, 128, WB_COLS], F32, kind="ExternalInput").ap()
    prm_d = nc.dram_tensor("prm", [128, NPRM], F32, kind="ExternalInput").ap()
    pm_d = nc.dram_tensor("pm", [128, 12 * 128], F32, kind="ExternalInput").ap()
    out_d = nc.dram_tensor("out", [128, NBLK - 1, NQ, 128], F32, kind="ExternalOutput").ap()
    wds_d = nc.dram_tensor("wds", [16, 128, 32 * 64], BF16, kind="Internal").ap()

    from contextlib import ExitStack
    with ExitStack() as es:
        def sb(name, shape, dt):
            return es.enter_context(nc.sbuf_tensor(name, shape, dt))

        def sem(name):
            return es.enter_context(nc.semaphore(name))

        x_sb = sb("x_sb", [128, XSLOTS, NQ, 128], F32)
        WA = sb("WA", [128, WA_COLS], BF16)
        WB = sb("WB", [128, WB_COLS], BF16)
        prm = sb("prm_sb", [128, NPRM], F32)
        PM = sb("PM", [128, 12, 128], BF16)
        ones = sb("ones", [128, 128], BF16)
        epsr = sb("epsr", [128, 1], F32)
        zero1 = sb("zero1", [128, 1], F32)
        bs = sb("bs", [128, 16], F32)
        hist = [sb(f"hist{j}", [128, NQ, H], BF16) for j in range(2)]
        prevT = [sb(f"prevT{j}", [128, D], BF16) for j in range(2)]
        hnb = sb("hnb", [128, NQ, 256], BF16)
        RW = H + 256 + 16
        RA = sb("RA", [128, NQ, 2, RW], BF16)
        szb = [sb(f"szb{i}", [128, NQ, 256], BF16) for i in range(2)]
        vb = sb("vb", [128, NQ, 256], BF16)
        gtb = sb("gtb", [128, NQ, 256], BF16)
        uT = sb("uT", [128, 2, D], BF16)
        Wd = [sb(f"Wd{i}", [128, 32, 64], BF16) for i in range(NWD)]
        Wd.append(uT[:].rearrange("p a (b c) -> p (a b) c", c=64))
        UT_ALIAS = ["uT0_0", "uT0_1", "uT1_0", "uT1_1"]
        st_a = sb("st_a", [128, 256], F32)
        st_b = sb("st_b", [128, 256], F32)
        st_c = sb("st_c", [128, 256], F32)
        tmp = [sb(f"tmp{i}", [128, 256], F32) for i in range(NTMP - 1)]
        tmp.append(prm[:, P_DW:P_DW + 256])
        ps = [es.enter_context(nc.psum_tensor(f"ps{i}", [128, 512], F32)) for i in range(8)]

        s_pe, s_act, s_dve, s_pool = sem("s_pe"), sem("s_act"), sem("s_dve"), sem("s_pool")
        d_const = _DSem(sem("d_const"))
        d_pm = _DSem(sem("d_pm"))
        d_wa = _DSem(sem("d_wa"))
        d_wb = _DSem(sem("d_wb"))
        d_x = [_DSem(sem(f"d_x{i}")) for i in range(XSLOTS)]
        d_out = [_DSem(sem(f"d_out{i}")) for i in range(4)]
        d_r = [_DSem(sem(f"d_r{i}")) for i in range(NQ)]
        d_prev = [_DSem(sem(f"d_prev{i}")) for i in range(2)]
        d_wdst = [_DSem(sem(f"d_wdst{i}")) for i in range(NWD)]
        d_wdld = [_DSem(sem(f"d_wdld{i}")) for i in range(NWD + 1)]

        res = defaultdict(_Res)
        PE = _Q(s_pe, res, inorder=True)
        ACT = _Q(s_act, res)
        DVE = _Q(s_dve, res)
        POOL = _Q(s_pool, res)
        SP = _Q(None, res)

        cnt = {"tmp": 0, "r": 0, "wd": 0}
        preload = {}

        def slot(b, h, N):
            return ps[b][:, h * 256:h * 256 + N]

        def pcol(c):
            return prm[:, c:c + 1]

        SP.dma_group(d_const, [lambda q: q.dma_start(out=prm[:], in_=prm_d[:])], writes=["prm"])
        POOL.dma_group(d_pm, [lambda q: q.dma_start(out=PM[:].rearrange("p a b -> p (a b)"), in_=pm_d[:])],
                       writes=["PM"])
        POOL.op(lambda q: q.memset(ones[:], 1.0 / D), writes=["ones"])
        POOL.op(lambda q: q.memset(epsr[:], RMS_EPS), writes=["eps"])
        POOL.op(lambda q: q.memset(zero1[:], 0.0), writes=["eps"])
        for j in range(2):
            POOL.op(lambda q, j=j: q.memset(hist[j][:], 0.0), writes=[f"hist{j}"])
            POOL.op(lambda q, j=j: q.memset(prevT[j][:], 0.0), writes=[f"prevT{j}"])
        POOL.op(lambda q: q.memset(RA[:], 0.0), writes=[f"uo{k}" for k in range(NQ)] + [f"uc{k}" for k in range(NQ)])
        DVE.op(lambda q: q.tensor_tensor(out=bs[:], in0=prm[:, P_BG:P_BG + 16], in1=prm[:, P_SC:P_SC + 16],
                                         op=ALU.mult), reads=["prm"], writes=["bs"])

        PIECE = 2048
        WA_RES = [f"WA{i}" for i in range(WA_COLS // PIECE)]
        WB_RES = [f"WB{i}" for i in range(WB_COLS // PIECE)]
        pending = []

        def queue_weights(kind, j):
            if kind == "A":
                for i in range(WA_COLS // PIECE):
                    pending.append(("A", j, i))
            else:
                for i in range(WB_COLS // PIECE):
                    pending.append(("B", j, i))

        def pump(n=1):
            for _ in range(n):
                if not pending:
                    return
                kind, j, i = pending.pop(0)
                a, b = i * PIECE, (i + 1) * PIECE
                if kind == "A":
                    POOL.dma_group(d_wa, [lambda q, a=a, b=b, j=j: q.dma_start(out=WA[:, a:b], in_=wa_d[j, :, a:b])],
                                   writes=[f"WA{i}"])
                else:
                    POOL.dma_group(d_wb, [lambda q, a=a, b=b, j=j: q.dma_start(out=WB[:, a:b], in_=wb_d[j, :, a:b])],
                                   writes=[f"WB{i}"])

        def xview(k0, nb):
            return x_sb[:, k0:k0 + nb, :, :]

        def rms_stage(L, k0, nb, xres):
            N = 128 * nb
            ACT.op(lambda q: q.activation(out=hnb[:, :, 0:N].rearrange("p q (b t) -> p b q t", b=nb),
                                          in_=xview(k0, nb), func=AF.Square),
                   reads=xres, writes=["hn"])
            PE.mm_group([dict(out=slot(6, 0, N), lhsT=ones[:], rhs=hnb[:, q, 0:N],
                              start=(q == 0), stop=(q == NQ - 1)) for q in range(NQ)],
                        reads=["hn", "ones"], writes=["ps6"], label="rms_ss")
            ACT.op(lambda q: q.activation(out=st_a[:, 0:N], in_=slot(6, 0, N), func=AF.Sqrt,
                                          bias=epsr[:, 0:1], scale=1.0),
                   reads=["ps6", "eps"], writes=["st_a"])

        def hn_stage(L, k0, nb, xres):
            N = 128 * nb
            DVE.op(lambda q: q.reciprocal(out=st_a[:, 0:N], in_=st_a[:, 0:N]), reads=["st_a"], writes=["st_a"])
            for qq in range(NQ):
                DVE.op(lambda q, qq=qq: q.scalar_tensor_tensor(
                    out=hnb[:, qq, 0:N].rearrange("p (b t) -> p b t", b=nb),
                    in0=x_sb[:, k0:k0 + nb, qq, :], scalar=pcol(P_GAIN + L * 8 + qq),
                    in1=st_a[:, 0:N].rearrange("p (b t) -> p b t", b=nb),
                    op0=ALU.mult, op1=ALU.mult),
                    reads=xres + ["st_a", "prm"], writes=["hn"])

        def outproj_stage(Wt, wres, base, k0, nb, xres, mask_halo, mid=None):
            N = 128 * nb
            for dq in range(NQ):
                if dq == 4 and mid is not None:
                    mid()
                b, h = (7 if dq % 2 == 0 else 5), 0
                PE.mm_group([dict(out=slot(b, h, N), lhsT=Wt[:, base + eq * 1024 + dq * 128: base + eq * 1024 + dq * 128 + 128],
                                  rhs=gtb[:, eq, 0:N], start=(eq == 0), stop=(eq == NQ - 1)) for eq in range(NQ)],
                            reads=wres + [f"gt{e}" for e in range(NQ)], writes=[f"ps{b}"], label=f"outproj{dq}")
                DVE.op(lambda q, dq=dq, b=b, h=h: q.tensor_tensor(
                    out=x_sb[:, k0:k0 + nb, dq, :], in0=x_sb[:, k0:k0 + nb, dq, :],
                    in1=slot(b, h, N).rearrange("p (b t) -> p b t", b=nb), op=ALU.add),
                    reads=[f"ps{b}"] + xres, writes=xres)
            if mask_halo:
                DVE.op(lambda q: q.tensor_tensor(
                    out=x_sb[:, 0, :, :], in0=x_sb[:, 0, :, :],
                    in1=prm[:, P_MASK:P_MASK + 128].unsqueeze(1).to_broadcast([128, NQ, 128]), op=ALU.mult),
                    reads=["x0", "prm"], writes=["x0"])

        class Item:
            pass

        def xres_of(it):
            return [f"x{s}" for s in range(it.k0, it.k0 + it.nb)]

        def phase1a(it):
            rms_stage(it.L, it.k0, it.nb, xres_of(it))

        def phase1b(it):
            xres = xres_of(it)
            k0, nb, N = it.k0, it.nb, 128 * it.nb
            if it.kind != "final":
                hn_stage(it.L, k0, nb, xres)
                return
            DVE.op(lambda q: q.reciprocal(out=st_a[:, 0:N], in_=st_a[:, 0:N]), reads=["st_a"], writes=["st_a"])
            for qq in range(NQ):
                DVE.op(lambda q, qq=qq: q.scalar_tensor_tensor(
                    out=x_sb[:, k0:k0 + nb, qq, :], in0=x_sb[:, k0:k0 + nb, qq, :], scalar=pcol(P_GAIN + 32 + qq),
                    in1=st_a[:, 0:N].rearrange("p (b t) -> p b t", b=nb), op0=ALU.mult, op1=ALU.mult),
                    reads=xres + ["st_a", "prm"], writes=xres)
            SP.dma_group(d_out[it.gidx], [lambda q: q.dma_start(out=out_d[:, it.blk0 - 1:it.blk0 - 1 + nb, :, :],
                                                                in_=x_sb[:, k0:k0 + nb, :, :])],
                         reads=xres, writes=[])
            if it.t + 1 < len(TILES):
                slots = list(range(k0, k0 + nb))
                if it.t == 0 and it.gidx == 1:
                    slots = [0] + slots
                for s_ in slots:
                    if s_ < len(TILES[it.t + 1]):
                        load_x_slot(it.t + 1, s_)

        def phase2(it, fill):
            nunits = 8 if it.kind == "conv" else (2 * it.nb + 8)
            nfill = len(fill)
            state = {"u": 0}

            def pop():
                pump(1)
                state["u"] += 1
                target = nfill if state["u"] >= nunits - 2 else (nfill * state["u"] + nunits - 3) // max(1, nunits - 2)
                while fill and nfill - len(fill) < target:
                    fill.pop(0)()
            if it.kind == "final":
                while fill:
                    pop()
                return
            j = it.L // 2
            k0, nb, N = it.k0, it.nb, 128 * it.nb
            sz = szb[it.par]
            szr = f"sz{it.par}_"
            if it.kind == "conv":
                ures = [f"uo{q}" for q in range(NQ)]
                POOL.op(lambda q: q.tensor_copy(out=RA[0:64, :, 0, 0:H], in_=hist[j][0:64, :, :]),
                        reads=[f"hist{j}"], writes=ures)
                POOL.op(lambda q: q.tensor_copy(out=RA[64:128, :, 1, 16:16 + H], in_=hist[j][64:128, :, :]),
                        reads=[f"hist{j}"], writes=ures)
                for qq in range(NQ):
                    abk = (0, 1, 4)[qq % 3]
                    zbk = (2, 3, 5)[qq % 3]
                    for part, bank, half in ((0, abk, 0), (1, abk, 1), (2, zbk, 0)):
                        c0 = part * 1024 + qq * 128
                        PE.mm_group([dict(out=slot(bank, half, N), lhsT=WA[:, kq * 3072 + c0: kq * 3072 + c0 + 128],
                                          rhs=hnb[:, kq, 0:N], start=(kq == 0), stop=(kq == NQ - 1))
                                     for kq in range(NQ)], reads=["hn"] + WA_RES, writes=[f"ps{bank}"], label=f"c.inproj{qq}.{part}")
                    ti = cnt["tmp"] % NTMP
                    cnt["tmp"] += 1
                    ACT.op(lambda q, ti=ti, abk=abk: q.activation(out=tmp[ti][:, 0:N], in_=slot(abk, 1, N),
                                                                 func=AF.Tanh, scale=0.5),
                           reads=[f"ps{abk}"], writes=[f"tmp{ti}"])
                    DVE.op(lambda q, ti=ti, abk=abk, qq=qq: q.scalar_tensor_tensor(
                        out=RA[0:64, qq, 0, H:H + N], in0=tmp[ti][0:64, 0:N], scalar=1.0, in1=slot(abk, 0, N)[0:64, :],
                        op0=ALU.add, op1=ALU.mult),
                        reads=[f"tmp{ti}", f"ps{abk}"], writes=[f"uo{qq}"])
                    DVE.op(lambda q, ti=ti, abk=abk, qq=qq: q.scalar_tensor_tensor(
                        out=RA[64:128, qq, 1, H + 16:H + 16 + N], in0=tmp[ti][64:128, 0:N], scalar=1.0,
                        in1=slot(abk, 0, N)[64:128, :], op0=ALU.add, op1=ALU.mult),
                        reads=[f"tmp{ti}", f"ps{abk}"], writes=[f"uo{qq}"])
                    SP.dma_group(d_r[qq], [
                        lambda q, qq=qq: q.dma_start(out=RA[0:64, qq, 1, 0:H + N], in_=RA[64:128, qq, 1, 16:16 + H + N]),
                        lambda q, qq=qq: q.dma_start(out=RA[64:128, qq, 0, 16:16 + H + N], in_=RA[0:64, qq, 0, 0:H + N])],
                        reads=[f"uo{qq}"], writes=[f"uc{qq}"])
                    ACT.op(lambda q, zbk=zbk, qq=qq: q.activation(out=sz[:, qq, 0:N], in_=slot(zbk, 0, N), func=AF.Silu),
                           reads=[f"ps{zbk}"], writes=[szr + str(qq)])
                    pop()
                POOL.op(lambda q: q.tensor_copy(out=hist[j][0:64, :, :], in_=RA[0:64, :, 0, N:N + H]),
                        reads=ures, writes=[f"hist{j}"])
                POOL.op(lambda q: q.tensor_copy(out=hist[j][64:128, :, :], in_=RA[64:128, :, 1, N + 16:N + 16 + H]),
                        reads=ures, writes=[f"hist{j}"])
            else:
                for bb in range(nb):
                    for hh in range(2):
                        bank = (0, 1, 4)[(bb * 2 + hh) % 3]
                        PE.mm_group([dict(out=ps[bank][:, :], lhsT=hnb[:, kq, bb * 128:(bb + 1) * 128],
                                          rhs=WB[:, kq * 2048 + hh * 512: kq * 2048 + hh * 512 + 512],
                                          start=(kq == 0), stop=(kq == NQ - 1)) for kq in range(NQ)],
                                    reads=["hn"] + WB_RES, writes=[f"ps{bank}"], label=f"p.uT{bb}{hh}")
                        ACT.op(lambda q, bb=bb, hh=hh, bank=bank: q.activation(
                            out=uT[:, bb, hh * 512:(hh + 1) * 512], in_=ps[bank][:, :], func=AF.Copy),
                            reads=[f"ps{bank}"], writes=[f"uT{bb}_{hh}"])
                        pop()
                for qq in range(NQ):
                    zbk = (2, 3, 5)[qq % 3]
                    c0 = 1024 + qq * 128
                    PE.mm_group([dict(out=slot(zbk, 0, N), lhsT=WB[:, kq * 2048 + c0: kq * 2048 + c0 + 128],
                                      rhs=hnb[:, kq, 0:N], start=(kq == 0), stop=(kq == NQ - 1)) for kq in range(NQ)],
                                reads=["hn"] + WB_RES, writes=[f"ps{zbk}"], label=f"p.z{qq}")
                    ACT.op(lambda q, zbk=zbk, qq=qq: q.activation(out=sz[:, qq, 0:N], in_=slot(zbk, 0, N), func=AF.Silu),
                           reads=[f"ps{zbk}"], writes=[szr + str(qq)])
                    pop()
            while fill:
                pop()

        def phase3(it, mid):
            if it.kind == "final":
                mid()
                return
            if it.kind == "conv":
                mid()
            j = it.L // 2
            k0, nb, N = it.k0, it.nb, 128 * it.nb
            sz = szb[it.par]
            szr = f"sz{it.par}_"
            if it.kind == "conv":
                wsel = preload.pop(id(it), {})

                def wd_load(qq, jj=j, sel=None):
                    sel = wsel if sel is None else sel
                    wp = cnt["wd"] % (NWD + 1)
                    cnt["wd"] += 1
                    wdres = [f"Wd{wp}"] + (UT_ALIAS if wp == NWD else [])
                    sel[qq] = (wp, wdres)
                    widx = jj * 8 + qq
                    SP.dma_group(d_wdld[wp], [lambda q, wp=wp, widx=widx, hf=hf: q.dma_start(
                        out=Wd[wp][:, 16 * hf:16 * hf + 16, :].rearrange("p a b -> p (a b)"),
                        in_=wds_d[widx][:, 1024 * hf:1024 * hf + 1024]) for hf in range(2)],
                        reads=[f"wds{widx}"], writes=wdres)

                for qq in range(4):
                    if qq not in wsel:
                        wd_load(qq)
                for qq in range(NQ):
                    par = qq % 2
                    wp, wdres = wsel[qq]
                    mms = []
                    for kk in range(16):
                        for m in range(2):
                            mms.append(dict(out=slot(4 + par, 0, N)[64 * m:64 * m + 64, :], lhsT=Wd[wp][:, m * 16 + kk, :],
                                            rhs=RA[:, qq, m, H - kk:H - kk + N], start=(kk == 0), stop=(kk == 15),
                                            skip_group_check=True, tile_position=(0, 64 * m)))
                    PE.mm_group(mms, reads=[f"uo{qq}", f"uc{qq}"] + wdres, writes=[f"ps{4 + par}"], label=f"c.conv{qq}")
                    if qq + 4 < NQ:
                        wd_load(qq + 4)
                    else:
                        nci = it.next_conv
                        if nci is not None and (nci is it.next_item or qq < 7):
                            wd_load(qq - 4, jj=nci.L // 2, sel=preload.setdefault(id(nci), {}))
                    ACT.op(lambda q, par=par, qq=qq: q.activation(out=vb[:, qq, 0:N], in_=slot(4 + par, 0, N), func=AF.Identity,
                                                                 bias=pcol(P_DWB + j * 8 + qq), scale=1.0),
                           reads=[f"ps{4 + par}", "prm"], writes=[f"v{qq}"])
                    ACT.op(lambda q, par=par, qq=qq: q.activation(out=gtb[:, qq, 0:N], in_=slot(4 + par, 0, N), func=AF.Square,
                                                                 bias=pcol(P_DWB + j * 8 + qq), scale=1.0),
                           reads=[f"ps{4 + par}", "prm"], writes=[f"gt{qq}"])
                vres = [f"v{q}" for q in range(NQ)]
                gres = [f"gt{q}" for q in range(NQ)]
                PE.mm_group([dict(out=slot(6, 0, N), lhsT=ones[:], rhs=vb[:, q, 0:N], start=(q == 0), stop=(q == NQ - 1))
                             for q in range(NQ)], reads=vres + ["ones"], writes=["ps6"], label="c.ln_s1")
                PE.mm_group([dict(out=slot(6, 1, N), lhsT=ones[:], rhs=gtb[:, q, 0:N], start=(q == 0), stop=(q == NQ - 1))
                             for q in range(NQ)], reads=gres + ["ones"], writes=["ps6"], label="c.ln_s2")
            else:
                for qq in range(NQ):
                    par = qq % 2
                    dbk = (4, 5, 7, 6)[qq % 4]
                    gi = qq // 2
                    hh = qq // 4
                    mms = []
                    rd = ["PM"]
                    for bb in range(nb):
                        kind = 2 if (it.first_blk and bb == 0) else 0
                        o = slot(dbk, 0, N)[:, bb * 128:(bb + 1) * 128]
                        mms.append(dict(out=o, lhsT=uT[:, bb, qq * 128:(qq + 1) * 128], rhs=PM[:, kind * 4 + gi, :],
                                        start=True, stop=False))
                        if bb == 0:
                            pl = prevT[j][:, qq * 128:(qq + 1) * 128]
                            rd.append(f"prevT{j}")
                        else:
                            pl = uT[:, bb - 1, qq * 128:(qq + 1) * 128]
                        mms.append(dict(out=o, lhsT=pl, rhs=PM[:, 4 + gi, :], start=False, stop=True))
                        rd.append(f"uT{bb}_{hh}")
                    PE.mm_group(mms, reads=rd, writes=[f"ps{dbk}"], label=f"p.pool{qq}")
                    ACT.op(lambda q, dbk=dbk, qq=qq: q.activation(out=vb[:, qq, 0:N], in_=slot(dbk, 0, N), func=AF.Copy),
                           reads=[f"ps{dbk}"], writes=[f"v{qq}"])
                mid()
                SP.dma_group(d_prev[j], [lambda q: q.dma_start(out=prevT[j][:, :], in_=uT[:, nb - 1, :])],
                             reads=[f"uT{nb - 1}_0", f"uT{nb - 1}_1"], writes=[f"prevT{j}"])
                for cq in range(NQ):
                    par = cq % 2
                    gi = cq // 2
                    mmi = cq % 2
                    base = WB_GRP + gi * 512
                    yb = cq % 4
                    PE.mm_group([dict(out=slot(yb, 0, N), lhsT=WB[:, base + kk * 256 + mmi * 128: base + kk * 256 + mmi * 128 + 128],
                                      rhs=vb[:, 2 * gi + kk, 0:N], start=(kk == 0), stop=(kk == 1)) for kk in range(2)],
                                reads=WB_RES + [f"v{2 * gi}", f"v{2 * gi + 1}"], writes=[f"ps{yb}"], label=f"p.grp{cq}")
                    ACT.op(lambda q, yb=yb, cq=cq: q.activation(
                        out=gtb[:, cq, 0:N], in_=slot(yb, 0, N), func=AF.Identity,
                        bias=bs[:, j * 8 + cq:j * 8 + cq + 1], scale=pcol(P_SC + j * 8 + cq)),
                        reads=[f"ps{yb}", "bs", "prm"], writes=[f"gt{cq}"])

        def phase4_steps(it):
            if it.kind == "final":
                return []
            j = it.L // 2
            N = 128 * it.nb
            sz = szb[it.par]
            szr = f"sz{it.par}_"
            if it.kind == "pool":
                def mkp(h):
                    def step():
                        DVE.op(lambda q: q.tensor_tensor(out=gtb[:, 4 * h:4 * h + 4, 0:N], in0=gtb[:, 4 * h:4 * h + 4, 0:N],
                                                         in1=sz[:, 4 * h:4 * h + 4, 0:N], op=ALU.mult),
                               reads=[f"gt{c}" for c in range(4 * h, 4 * h + 4)] + [szr + str(c) for c in range(4 * h, 4 * h + 4)],
                               writes=[f"gt{c}" for c in range(4 * h, 4 * h + 4)])
                    return step
                return [mkp(0), mkp(1)]

            def chain():
                ACT.op(lambda q: q.activation(out=st_b[:, 0:N], in_=slot(6, 0, N), func=AF.Square),
                       reads=["ps6"], writes=["st_b"])
                DVE.op(lambda q: q.scalar_tensor_tensor(out=st_c[:, 0:N], in0=slot(6, 1, N), scalar=LN_EPS,
                                                        in1=st_b[:, 0:N], op0=ALU.add, op1=ALU.subtract),
                       reads=["ps6", "st_b"], writes=["st_c"])
                ACT.op(lambda q: q.activation(out=st_c[:, 0:N], in_=st_c[:, 0:N], func=AF.Sqrt,
                                              bias=zero1[:, 0:1], scale=1.0),
                       reads=["st_c", "eps"], writes=["st_c"])
                DVE.op(lambda q: q.reciprocal(out=st_c[:, 0:N], in_=st_c[:, 0:N]), reads=["st_c"], writes=["st_c"])
                DVE.op(lambda q: q.scalar_tensor_tensor(out=st_b[:, 0:N], in0=slot(6, 0, N), scalar=-1.0,
                                                        in1=st_c[:, 0:N], op0=ALU.mult, op1=ALU.mult),
                       reads=["ps6", "st_c"], writes=["st_b"])

            tmps = {}

            def A(qq):
                ti = cnt["tmp"] % NTMP
                cnt["tmp"] += 1
                tmps[qq] = ti
                DVE.op(lambda q: q.tensor_tensor(out=tmp[ti][:, 0:N], in0=vb[:, qq, 0:N],
                                                 in1=st_c[:, 0:N], op=ALU.mult),
                       reads=[f"v{qq}", "st_c"], writes=[f"tmp{ti}"])
                DVE.op(lambda q: q.tensor_tensor(out=tmp[ti][:, 0:N], in0=tmp[ti][:, 0:N],
                                                 in1=st_b[:, 0:N], op=ALU.add),
                       reads=[f"tmp{ti}", "st_b"], writes=[f"tmp{ti}"])

            def B(qq):
                ti = tmps[qq]
                ACT.op(lambda q: q.activation(out=gtb[:, qq, 0:N], in_=tmp[ti][:, 0:N], func=AF.Silu,
                                              bias=pcol(P_LNB + j * 8 + qq), scale=pcol(P_LNG + j * 8 + qq)),
                       reads=[f"tmp{ti}", "prm"], writes=[f"gt{qq}"])

            def C(qq):
                DVE.op(lambda q: q.tensor_tensor(out=gtb[:, qq, 0:N], in0=gtb[:, qq, 0:N],
                                                 in1=sz[:, qq, 0:N], op=ALU.mult),
                       reads=[f"gt{qq}", szr + str(qq)], writes=[f"gt{qq}"])

            def mk(e):
                def step():
                    if 0 <= e - 2 < NQ:
                        C(e - 2)
                    if 0 <= e - 1 < NQ:
                        B(e - 1)
                    if 0 <= e < NQ:
                        A(e)
                return step
            return [chain] + [mk(e) for e in range(NQ + 2)]

        def phase5(it, mid=None):
            if it.kind == "final":
                if mid is not None:
                    mid()
                return
            if it.kind == "conv":
                outproj_stage(WA, WA_RES, WA_OUT, it.k0, it.nb, xres_of(it), it.mask_halo, mid)
            else:
                outproj_stage(WB, WB_RES, WB_OUT, it.k0, it.nb, xres_of(it), it.mask_halo, mid)

        items = []
        nlay = DBG_LAYERS
        for t, blocks in enumerate(TILES):
            groups = _groups(blocks)
            for L in list(range(nlay)) + [4]:
                for gi, g in enumerate(groups):
                    it = Item()
                    it.kind = "final" if L == 4 else ("conv" if L % 2 == 0 else "pool")
                    it.L, it.t, it.gidx = L, t, gi
                    it.k0, it.nb, it.blk0 = g[0] - blocks[0], len(g), g[0]
                    it.mask_halo = (g[0] == 0)
                    it.first_blk = (g[0] == 1)
                    it.par = len(items) % 2
                    it.last_of_step = (gi == len(groups) - 1)
                    it.first_of_tile = (L == 0 and gi == 0)
                    if it.kind == "final" and g[0] == 0:
                        continue
                    items.append(it)

        last_conv = None
        nxt_item = None
        for it_ in reversed(items):
            it_.next_conv = last_conv
            it_.next_item = nxt_item
            nxt_item = it_
            if it_.kind == "conv":
                last_conv = it_

        def load_x_slot(t, s_):
            blk = TILES[t][s_]
            SP.dma_group(d_x[s_], [lambda q: q.dma_start(out=x_sb[:, s_, :, :], in_=x_d[:, blk, :, :])],
                         writes=[f"x{s_}"])

        def load_x(t):
            for s_ in range(len(TILES[t])):
                load_x_slot(t, s_)

        step_list = [(t, L) for t in range(len(TILES)) for L in range(nlay)]

        def after_phase5(it):
            if it.kind == "final" or not it.last_of_step:
                return
            si = step_list.index((it.t, it.L))
            pump(1000)
            if si + 2 < len(step_list):
                t2, L2 = step_list[si + 2]
                queue_weights("A" if L2 % 2 == 0 else "B", L2 // 2)

        queue_weights("A", 0)
        pump(1000)
        for idx in range(16):
            wp = idx % NWD
            dwc = P_DW + idx * 32
            POOL.op(lambda q, wp=wp, dwc=dwc: q.tensor_tensor(
                out=Wd[wp][:], in0=prm[:, P_ID:P_ID + 64].unsqueeze(1).to_broadcast([128, 32, 64]),
                in1=prm[:, dwc:dwc + 32].unsqueeze(2).to_broadcast([128, 32, 64]), op=ALU.mult),
                reads=["prm"], writes=[f"Wd{wp}"])
            SP.dma_group(d_wdst[wp], [lambda q, wp=wp, idx=idx: q.dma_start(
                out=wds_d[idx], in_=Wd[wp][:].rearrange("p a b -> p (a b)"))],
                reads=[f"Wd{wp}"], writes=[f"wds{idx}"])

        res[f"tmp{NTMP - 1}"].r[id(s_pool)] = (s_pool, POOL.cnt)
        if nlay > 1:
            queue_weights("B", 0)
        pump(1000)
        load_x(0)
        phase1a(items[0])
        phase1b(items[0])
        phase2(items[0], [])
        if len(items) > 1:
            phase1a(items[1])
            phase1b(items[1])
        for i, it in enumerate(items):
            nxt = items[i + 1] if i + 1 < len(items) else None
            nxt2 = items[i + 2] if i + 2 < len(items) else None
            phase3(it, lambda: None)
            fill = phase4_steps(it)
            if nxt is not None:
                phase2(nxt, fill)
            while fill:
                fill.pop(0)()
            phase5(it, (lambda: phase1a(nxt2)) if nxt2 is not None else None)
            if nxt2 is not None:
                phase1b(nxt2)
            after_phase5(it)
        for ds in d_out + d_prev + d_r + d_wdld + d_wdst + d_x + [d_wa, d_wb, d_const, d_pm]:
            if ds.cnt:
                SP.prog.append(("w", ds.sem, ds.cnt))

        _LAST["PE"] = PE.prog
        with nc.Block() as block:
            @block.tensor
            def _(q):
                PE.replay(q)

            @block.scalar
            def _(q):
                ACT.replay(q)

            @block.vector
            def _(q):
                DVE.replay(q)

            @block.gpsimd
            def _(q):
                POOL.replay(q)

            @block.sync
            def _(q):
                SP.replay(q)
    return nc


def _prep_core_inputs(c, x, prm_common, wa, wb):
    b, s0 = c // 4, (c % 4) * TOK_PER_CORE
    xc = np.zeros((NBLK * 128, D), np.float32)
    lo = s0 - 128
    if lo < 0:
        xc[128:] = x[b, 0:TOK_PER_CORE]
    else:
        xc[:] = x[b, lo:s0 + TOK_PER_CORE]
    xl = np.ascontiguousarray(xc.reshape(NBLK, 128, NQ, 128).transpose(3, 0, 2, 1))
    prm = prm_common.copy()
    start = (lo < 0)
    prm[:, P_MASK:P_MASK + 128] = 0.0 if start else 1.0
    pm = np.zeros((128, 3, 4, 128), np.float32)
    tt = np.arange(128)
    for gi, w in enumerate(WINDOWS):
        dlt = tt[None, :] - tt[:, None]
        band = ((dlt >= 0) & (dlt < w)).astype(np.float32)
        pm[:, 0, gi, :] = band / w - np.eye(128, dtype=np.float32)
        dprev = tt[None, :] + 128 - tt[:, None]
        pm[:, 1, gi, :] = (dprev < w).astype(np.float32) / w
        if start:
            cntv = np.minimum(tt + 1, w).astype(np.float32)
            pm[:, 2, gi, :] = band / cntv[None, :] - np.eye(128, dtype=np.float32)
        else:
            pm[:, 2, gi, :] = pm[:, 0, gi, :]
    return {"x": xl, "wa": wa, "wb": wb, "prm": prm, "pm": np.ascontiguousarray(pm.reshape(128, 12 * 128))}


def _pq(v):
    return np.ascontiguousarray(np.asarray(v, np.float32).reshape(NQ, 128).T)


def kernel(x, norm_g, final_g, conv_w_in, conv_dw, conv_dw_b, conv_ln_g, conv_ln_b, conv_w_out,
           pool_w_in, pool_w_grp, pool_b_grp, pool_scale, pool_w_out):
    x = np.asarray(x, np.float32)
    prm = np.zeros((128, NPRM), np.float32)
    for l in range(4):
        prm[:, P_GAIN + l * 8:P_GAIN + l * 8 + 8] = _pq(norm_g[l])
    prm[:, P_GAIN + 32:P_GAIN + 40] = _pq(final_g)
    for j in range(2):
        prm[:, P_DWB + j * 8:P_DWB + j * 8 + 8] = _pq(conv_dw_b[j])
        prm[:, P_LNG + j * 8:P_LNG + j * 8 + 8] = _pq(conv_ln_g[j])
        prm[:, P_LNB + j * 8:P_LNB + j * 8 + 8] = _pq(conv_ln_b[j])
        prm[:, P_BG + j * 8:P_BG + j * 8 + 8] = _pq(pool_b_grp[j])
        prm[:, P_SC + j * 8:P_SC + j * 8 + 8] = _pq(pool_scale[j])
        dwp = np.zeros((32, D), np.float32)
        dwp[:31] = np.asarray(conv_dw[j], np.float32)[::-1]
        t5 = dwp.reshape(2, 16, NQ, 2, 64).transpose(0, 4, 2, 3, 1)
        prm[:, P_DW + j * 256:P_DW + j * 256 + 256] = t5.reshape(128, 256)
    prm[:, P_ID:P_ID + 64] = 0.5 * np.concatenate([np.eye(64, dtype=np.float32)] * 2, 0)
    wa = np.zeros((2, 128, WA_COLS), np.float32)
    wb = np.zeros((2, 128, WB_COLS), np.float32)
    for j in range(2):
        wa[j, :, :WA_OUT] = np.asarray(conv_w_in[j], np.float32).reshape(NQ, 128, 3072).transpose(1, 0, 2).reshape(128, -1)
        wa[j, :, WA_OUT:] = np.asarray(conv_w_out[j], np.float32).reshape(NQ, 128, 1024).transpose(1, 0, 2).reshape(128, -1)
        wb[j, :, :WB_GRP] = np.asarray(pool_w_in[j], np.float32).reshape(NQ, 128, 2048).transpose(1, 0, 2).reshape(128, -1)
        wb[j, :, WB_GRP:WB_OUT] = np.asarray(pool_w_grp[j], np.float32).reshape(4, 2, 128, 256).transpose(2, 0, 1, 3).reshape(128, -1)
        wb[j, :, WB_OUT:] = np.asarray(pool_w_out[j], np.float32).reshape(NQ, 128, 1024).transpose(1, 0, 2).reshape(128, -1)
    in_maps = [_prep_core_inputs(c, x, prm, wa, wb) for c in range(NCORES)]
    nc = build_program()
    res = run_bass_kernel_spmd(nc, in_maps, core_ids=list(range(NCORES)))
    out = np.empty((2, SEQ, D), np.float32)
    for c in range(NCORES):
        b, s0 = c // 4, (c % 4) * TOK_PER_CORE
        oc = np.asarray(res.results[c]["out"], np.float32)
        out[b, s0:s0 + TOK_PER_CORE] = oc.transpose(1, 3, 2, 0).reshape(TOK_PER_CORE, D)
    return out
```
